# Optimizing a Trainium2 kernel written in Bass

```python
import jax, jax.numpy as jnp
from jax import lax
import numpy as np

D_MODEL = 2048
BATCH = 4
SEQ = 2048
DEPTH = 2
DEC_BATCH = 128
DEC_SEQ = 1
PAST_LEN = 16384
PAGE_SIZE = 128

HEAD_DIM = 128
POOL_WIDTH = D_MODEL // 4
POOL_WINDOWS = (2, 4, 8, 16)
POOL_GW = POOL_WIDTH // len(POOL_WINDOWS)
POOL_BUF = max(POOL_WINDOWS) - 1
CONV_WIDTH = (D_MODEL - POOL_WIDTH) // 2
CONF_K = 31
SC_WIDTH = D_MODEL - POOL_WIDTH - CONV_WIDTH
SC_K = 3
FFN_K = 3
D_FF = ((8 * D_MODEL // 3 + 255) // 256) * 256
PLE_DIM = 256
IN_COLS = POOL_WIDTH + 2 * CONV_WIDTH + 3 * SC_WIDTH
EPS = 1e-6

kernel_name = "hybrid_pool_conformer_shortconv_decoder_step"


def rmsnorm(x, g):
    xf = x.astype(jnp.float32)
    y = xf * lax.rsqrt(jnp.mean(xf * xf, axis=-1, keepdims=True) + EPS)
    return (y * g.astype(jnp.float32)).astype(x.dtype)


def layernorm(x, g, b):
    xf = x.astype(jnp.float32)
    mu = jnp.mean(xf, axis=-1, keepdims=True)
    var = jnp.mean(jnp.square(xf - mu), axis=-1, keepdims=True)
    y = (xf - mu) * lax.rsqrt(var + EPS) * g.astype(jnp.float32) + b.astype(jnp.float32)
    return y.astype(x.dtype)


def causal_dwconv(buf, x, w):
    k = w.shape[0]
    xp = jnp.concatenate([buf.astype(x.dtype), x], axis=1)
    y = lax.conv_general_dilated(
        xp, w.astype(x.dtype)[:, None, :], window_strides=(1,), padding='VALID',
        dimension_numbers=('NWC', 'WIO', 'NWC'), feature_group_count=x.shape[-1])
    return y, xp[:, -(k - 1):]


def pool_mixer(u, buf, start, w_pool, scale):
    b, s, c = u.shape
    cat = jnp.concatenate([buf.astype(u.dtype), u], axis=1)
    cs = jnp.cumsum(cat.astype(jnp.float32), axis=1)
    cs = jnp.concatenate([jnp.zeros((b, 1, c), jnp.float32), cs], axis=1)
    pos = start + jnp.arange(s, dtype=jnp.int32)
    means = []
    for g, w in enumerate(POOL_WINDOWS):
        sl = slice(g * POOL_GW, (g + 1) * POOL_GW)
        wsum = cs[:, POOL_BUF + 1:, sl] - cs[:, POOL_BUF + 1 - w:POOL_BUF + 1 - w + s, sl]
        cnt = jnp.minimum(w, pos + 1).astype(jnp.float32)[None, :, None]
        means.append(wsum / cnt)
    d = (jnp.concatenate(means, axis=-1) - u.astype(jnp.float32)).astype(u.dtype)
    d = d.reshape(b, s, len(POOL_WINDOWS), POOL_GW)
    y = jnp.einsum('bsgc,gcd->bsgd', d, w_pool).reshape(b, s, c) * scale
    return y, cat[:, -POOL_BUF:]


def trunk_layer(h, p_i, st_pool, st_conf, st_sc, st_ffn, start,
                norm_mix, w_in, w_pool, pool_scale, conf_dw, conf_dw_b, conf_ln_g, conf_ln_b,
                conf_pw, conf_pw_b, sc_conv, w_out, norm_ffn, w_up, ffn_conv, w_down,
                norm_ple, ple_gate, ple_proj):
    hn = rmsnorm(h, norm_mix)
    z = hn @ w_in
    o = 0
    u_a = z[..., o:o + POOL_WIDTH]; o += POOL_WIDTH
    glu_a = z[..., o:o + CONV_WIDTH]; o += CONV_WIDTH
    glu_b = z[..., o:o + CONV_WIDTH]; o += CONV_WIDTH
    sc_b = z[..., o:o + SC_WIDTH]; o += SC_WIDTH
    sc_c = z[..., o:o + SC_WIDTH]; o += SC_WIDTH
    sc_h = z[..., o:o + SC_WIDTH]

    y_a, new_pool = pool_mixer(u_a, st_pool, start, w_pool, pool_scale)

    g = glu_a * jax.nn.sigmoid(glu_b)
    cb, new_conf = causal_dwconv(st_conf, g, conf_dw)
    cb = layernorm(cb + conf_dw_b, conf_ln_g, conf_ln_b)
    y_b = jax.nn.silu(cb) @ conf_pw + conf_pw_b

    v = sc_c * sc_h
    cv, new_sc = causal_dwconv(st_sc, v, sc_conv)
    y_c = sc_b * cv

    h = h + jnp.concatenate([y_a, y_b, y_c], axis=-1) @ w_out

    hn = rmsnorm(h, norm_ffn)
    up = hn @ w_up
    upc, new_ffn = causal_dwconv(st_ffn, up, ffn_conv)
    h = h + (jax.nn.silu(upc[..., :D_FF]) * upc[..., D_FF:]) @ w_down

    gate = jax.nn.sigmoid(rmsnorm(h, norm_ple) @ ple_gate)
    h = h + (p_i @ ple_proj) * gate
    return h, new_pool, new_conf, new_sc, new_ffn


def setup_inputs(seed: int = 0) -> dict:
    key = jax.random.key(seed)
    ks = jax.random.split(key, 32)
    f = jnp.float32
    nrm = lambda k, shape, s: jax.random.normal(k, shape, f) * s
    return {
        "x_prompt": nrm(ks[0], (BATCH, SEQ, D_MODEL), 1.0),
        "x_sample": nrm(ks[1], (DEC_BATCH, DEC_SEQ, D_MODEL), 1.0),
        "p_prompt": nrm(ks[2], (DEPTH, BATCH, SEQ, PLE_DIM), 1.0),
        "p_sample": nrm(ks[3], (DEPTH, DEC_BATCH, DEC_SEQ, PLE_DIM), 1.0),
        "state_pool": nrm(ks[4], (DEPTH, DEC_BATCH, POOL_BUF, POOL_WIDTH), 1.0),
        "state_conf": nrm(ks[5], (DEPTH, DEC_BATCH, CONF_K - 1, CONV_WIDTH), 0.5),
        "state_sc": nrm(ks[6], (DEPTH, DEC_BATCH, SC_K - 1, SC_WIDTH), 0.5),
        "state_ffn": nrm(ks[7], (DEPTH, DEC_BATCH, FFN_K - 1, 2 * D_FF), 1.0),
        "norm_mix": 1.0 + nrm(ks[8], (DEPTH, D_MODEL), 0.02),
        "w_in": nrm(ks[9], (DEPTH, D_MODEL, IN_COLS), D_MODEL ** -0.5),
        "w_pool": nrm(ks[10], (DEPTH, len(POOL_WINDOWS), POOL_GW, POOL_GW), POOL_GW ** -0.5),
        "pool_scale": 1.0 + nrm(ks[11], (DEPTH, POOL_WIDTH), 0.02),
        "conf_dw": nrm(ks[12], (DEPTH, CONF_K, CONV_WIDTH), CONF_K ** -0.5),
        "conf_dw_b": nrm(ks[13], (DEPTH, CONV_WIDTH), 0.02),
        "conf_ln_g": 1.0 + nrm(ks[14], (DEPTH, CONV_WIDTH), 0.02),
        "conf_ln_b": nrm(ks[15], (DEPTH, CONV_WIDTH), 0.02),
        "conf_pw": nrm(ks[16], (DEPTH, CONV_WIDTH, CONV_WIDTH), CONV_WIDTH ** -0.5),
        "conf_pw_b": nrm(ks[17], (DEPTH, CONV_WIDTH), 0.02),
        "sc_conv": nrm(ks[18], (DEPTH, SC_K, SC_WIDTH), SC_K ** -0.5),
        "w_out": nrm(ks[19], (DEPTH, D_MODEL, D_MODEL), D_MODEL ** -0.5),
        "norm_ffn": 1.0 + nrm(ks[20], (DEPTH, D_MODEL), 0.02),
        "w_up": nrm(ks[21], (DEPTH, D_MODEL, 2 * D_FF), D_MODEL ** -0.5),
        "ffn_conv": nrm(ks[22], (DEPTH, FFN_K, 2 * D_FF), FFN_K ** -0.5),
        "w_down": nrm(ks[23], (DEPTH, D_FF, D_MODEL), D_FF ** -0.5),
        "norm_ple": 1.0 + nrm(ks[24], (DEPTH, D_MODEL), 0.02),
        "ple_gate": nrm(ks[25], (DEPTH, D_MODEL, D_MODEL), D_MODEL ** -0.5),
        "ple_proj": nrm(ks[26], (DEPTH, PLE_DIM, D_MODEL), PLE_DIM ** -0.5),
        "norm_final": 1.0 + nrm(ks[27], (D_MODEL,), 0.02),
    }


def reference(x_prompt, x_sample, p_prompt, p_sample, state_pool, state_conf, state_sc, state_ffn,
              norm_mix, w_in, w_pool, pool_scale, conf_dw, conf_dw_b, conf_ln_g, conf_ln_b,
              conf_pw, conf_pw_b, sc_conv, w_out, norm_ffn, w_up, ffn_conv, w_down,
              norm_ple, ple_gate, ple_proj, norm_final):
    dt = x_prompt.dtype
    b = x_prompt.shape[0]
    z_pool = jnp.zeros((b, POOL_BUF, POOL_WIDTH), dt)
    z_conf = jnp.zeros((b, CONF_K - 1, CONV_WIDTH), dt)
    z_sc = jnp.zeros((b, SC_K - 1, SC_WIDTH), dt)
    z_ffn = jnp.zeros((b, FFN_K - 1, 2 * D_FF), dt)

    hp, hs = x_prompt, x_sample
    pp_l, ps_l, cp_l, cs_l, sp_l, ss_l, fp_l, fs_l = [], [], [], [], [], [], [], []
    for i in range(DEPTH):
        w = (norm_mix[i], w_in[i], w_pool[i], pool_scale[i], conf_dw[i], conf_dw_b[i],
             conf_ln_g[i], conf_ln_b[i], conf_pw[i], conf_pw_b[i], sc_conv[i], w_out[i],
             norm_ffn[i], w_up[i], ffn_conv[i], w_down[i], norm_ple[i], ple_gate[i], ple_proj[i])
        hp, a1, a2, a3, a4 = trunk_layer(hp, p_prompt[i], z_pool, z_conf, z_sc, z_ffn, 0, *w)
        hs, b1, b2, b3, b4 = trunk_layer(hs, p_sample[i], state_pool[i], state_conf[i],
                                         state_sc[i], state_ffn[i], PAST_LEN, *w)
        pp_l.append(a1); cp_l.append(a2); sp_l.append(a3); fp_l.append(a4)
        ps_l.append(b1); cs_l.append(b2); ss_l.append(b3); fs_l.append(b4)

    y_prompt = rmsnorm(hp, norm_final)
    y_sample = rmsnorm(hs, norm_final)
    return (y_prompt, y_sample,
            jnp.stack(pp_l), jnp.stack(ps_l),
            jnp.stack(cp_l), jnp.stack(cs_l),
            jnp.stack(sp_l), jnp.stack(ss_l),
            jnp.stack(fp_l), jnp.stack(fs_l))
```

```python
import numpy as np
from contextlib import ExitStack
import concourse.bass as bass
import concourse.mybir as mybir
from concourse.bass_utils import run_bass_kernel_spmd

F32 = mybir.dt.float32
BF16 = mybir.dt.bfloat16
AF = mybir.ActivationFunctionType
ALU = mybir.AluOpType
AX = mybir.AxisListType

L = 2
D = 2048
KC = 16
DFF = 5632
FC = 44
WP = 528
NS = 16
TPC = 1056
TC = TPC + NS
NCORE = 8
EPS = 1e-6
POOLW = (2, 4, 8, 16)
NWB = 5
GS = 11
NDT = 9
ENGS = ("pe", "act", "dve", "pool", "sp")

OFF_P, OFF_C, OFF_S, OFF_F, NST = 0, 60, 240, 252, 428


class Buf:
    def __init__(self, name, sem=None):
        self.name = name
        self.sem = sem
        self.w = None
        self.r = {}


class _Rec:
    def __init__(self):
        self.calls = []

    def __getattr__(self, name):
        def f(*a, **kw):
            self.calls.append((name, a, kw))
            return self
        return f


def _record(fn):
    r = _Rec()
    fn(r)
    return r.calls


class Sched:
    def __init__(self):
        self.q = {e: [] for e in ENGS}
        self.cnt = {e: 0 for e in ENGS}
        self.waited = {e: {} for e in ENGS}

    def _deps(self, reads, writes, extra):
        deps = list(extra)
        for b in reads:
            deps.append(b.w)
        for b in writes:
            deps.append(b.w)
            deps.extend(b.r.items())
        return deps

    def _waits(self, eng, deps):
        waits = []
        for d in deps:
            if d is None:
                continue
            s, v = d
            if v <= 0 or self.waited[eng].get(s, 0) >= v or (eng == "pe" and s == "pe"):
                continue
            self.waited[eng][s] = v
            waits.append((s, v))
        return waits

    def _mark(self, tok, reads, writes):
        for b in reads:
            b.r[tok[0]] = max(b.r.get(tok[0], 0), tok[1])
        for b in writes:
            b.w = tok
            b.r = {}

    def op(self, eng, fn, reads=(), writes=(), extra=()):
        waits = self._waits(eng, self._deps(reads, writes, extra))
        self.cnt[eng] += 1
        self.q[eng].append((_record(fn), waits, (eng, 1)))
        tok = (eng, self.cnt[eng])
        self._mark(tok, reads, writes)
        return tok

    def dma(self, eng, fn, sem, reads=(), writes=(), extra=()):
        waits = self._waits(eng, self._deps(reads, writes, extra))
        self.cnt[sem] = self.cnt.get(sem, 0) + 16
        self.q[eng].append((_record(fn), waits, (sem, 16)))
        tok = (sem, self.cnt[sem])
        self._mark(tok, reads, writes)
        return tok

    def wait(self, eng, deps):
        waits = self._waits(eng, deps)
        if waits:
            self.q[eng].append((None, waits, None))

    def barrier(self, engs=("pe", "act", "dve")):
        toks = [(o, self.cnt[o]) for o in ("pe", "act", "dve")]
        for e in engs:
            self.wait(e, toks)

    def emit(self, eng, e, sems):
        for fn, waits, inc in self.q[eng]:
            for s, v in waits:
                e.wait_ge(sems[s], v)
            if fn is not None:
                ins = None
                for name, a, kw in fn:
                    ins = getattr(e, name)(*a, **kw)
                if inc is not None:
                    ins.then_inc(sems[inc[0]], inc[1])


def _pack_consts():
    segs = [("nmix", L * 16), ("nffn", L * 16), ("nple", L * 16), ("nfin", 16), ("pscale", L * 4),
            ("cdw", L * 6 * 31), ("cdb", L * 6), ("lng", L * 6), ("lnb", L * 6), ("pwb", L * 6),
            ("scw", L * 6 * 3), ("ffw", L * FC * 2 * 3), ("invc", 64), ("ident", 128)]
    off, o = {}, 0
    for n, sz in segs:
        off[n] = o
        o += sz
    return off, o


COFF, NCST = _pack_consts()


def build_nc():
    nc = bass.Bass("TRN2", target_bir_lowering=False)
    dt_in = lambda n, s: nc.dram_tensor(n, s, F32, kind="ExternalInput").ap()
    dt_out = lambda n, s: nc.dram_tensor(n, s, F32, kind="ExternalOutput").ap()
    xT = dt_in("xT", [128, KC, TC])
    pT = dt_in("pT", [L, 128, 2, TC])
    hp_in = dt_in("hp", [L, 128, 4 * 16 * 16])
    hc_in = dt_in("hc", [L, 128, 6 * 31 * 16])
    hs_in = dt_in("hs", [L, 128, 6 * 3 * 16])
    hf_in = dt_in("hf", [L, 128, FC * 2 * 3 * 16])
    cst_in = dt_in("cst", [128, NCST])
    w_in = dt_in("w_in", [L * 34, 128, D])
    w_pool = dt_in("w_pool", [L * 4 * 128, 128])
    conf_pw = dt_in("conf_pw", [L * 6, 128, 768])
    w_out = dt_in("w_out", [L * KC, 128, D])
    w_up = dt_in("w_up", [L * 2 * FC, 128, D])
    w_down = dt_in("w_down", [L * KC, 128, DFF])
    ple_gate = dt_in("ple_gate", [L * KC, 128, D])
    ple_proj = dt_in("ple_proj", [L * 256, D])
    yT = dt_out("yT", [128, KC, TC])
    pst_out = dt_out("pst", [L, 128, NST])
    op_out = dt_out("ops", [L, 128, 4 * 16 * 16])
    oc_out = dt_out("ocs", [L, 128, 6 * 31 * 16])
    os_out = dt_out("oss", [L, 128, 6 * 3 * 16])
    of_out = dt_out("ofs", [L, 128, FC * 2 * 3 * 16])

    WMAX = WP + NS
    with ExitStack() as es:
        sb = lambda n, shp, dt=F32: es.enter_context(nc.sbuf_tensor(n, shp, dt))
        h_t = sb("h", [128, KC, WMAX])
        xn_t = sb("xn", [128, KC, WMAX], BF16)
        cat_t = sb("cat", [128, KC, WMAX], BF16)
        wb_t = sb("wb", [128, NWB, 2048], BF16)
        ptb_t = sb("ptb", [128, 2, WMAX], BF16)
        wpl_t = sb("wpl", [128, 4, 128], BF16)
        cst_t = sb("cstt", [128, NCST])
        carry_t = sb("carry", [128, L, NST])
        ost_t = sb("ost", [128, L, NST])
        zero_t = sb("zero", [128, 32])
        ones_t = sb("ones", [128, 128], BF16)
        ones_f = sb("onesf", [128, 2])
        warm_t = sb("warm", [128, 2])
        stp_t = sb("stp", [128, 4, 16, 16])
        stc_t = sb("stc", [128, 6, 31, 16])
        sts_t = sb("sts", [128, 6, 3, 16])
        stf_t = sb("stf", [128, FC, 2, 3, 16])
        rstd_t = sb("rstd", [128, WMAX])
        NU = 14848 + 1024 + 224 + 448
        U_t = sb("U", [128, NU])
        ps_t = es.enter_context(nc.psum_tensor("ps", [128, 8, 512], F32))

        dma_sems = ["s_y0", "s_y1", "s_hx0", "s_hx1", "s_hx2", "s_hx3", "s_h", "s_ptb", "s_wpl", "s_cst", "s_ptw", "s_ost0", "s_ost1", "s_stp", "s_stc", "s_sts", "s_stf"] + [f"s_w{i}" for i in range(NWB)]
        sems = {n: es.enter_context(nc.semaphore(n)) for n in list(ENGS) + dma_sems}
        block = es.enter_context(nc.Block())

        class Arena:
            def __init__(self):
                self.o = 0

            def f32(self, n):
                a = U_t[:, self.o:self.o + n]
                self.o += n
                assert self.o <= NU, ("arena overflow", self.o)
                return a

            def bf16(self, n):
                m = (n + 1) // 2
                return self.f32(m).bitcast(BF16)[:, 0:n]

        class WRing:
            def __init__(self):
                self.jobs = []
                self.dry = True

            def reset(self, S):
                self.S = S
                self.cur = 0
                self.rec = 0
                self.hold = 0
                self.bufs = [Buf(f"w{i}", f"s_w{i}") for i in range(NWB)]

            def next(self, mat, t, k0, nk):
                if self.dry:
                    self.jobs.append((mat, t, k0, nk))
                i = self.cur
                self.cur += 1
                assert self.jobs[i][1:] == (t, k0, nk)
                while self.rec < min(i + NWB - self.hold, len(self.jobs)):
                    n = self.rec
                    m_, t_, k0_, k_ = self.jobs[n]
                    slot = n % NWB
                    src = m_[t_, :, k0_ * 128:(k0_ + k_) * 128]
                    dst = wb_t[:, slot, 0:k_ * 128]
                    self.S.dma("pool", (lambda e, dst=dst, src=src: e.dma_start(out=dst, in_=src)),
                               f"s_w{slot}", writes=[self.bufs[slot]])
                    self.rec += 1
                slot = i % NWB
                view = wb_t[:, slot, 0:nk * 128].rearrange("p (k m) -> p k m", k=nk)
                return self.bufs[slot], view

        ring = WRing()

        def program(S):
            ring.reset(S)
            B = {n: Buf(n, s) for n, s in [("h", "s_h"), ("xn", None), ("cat", None), ("ptb", "s_ptb"), ("wpl", "s_wpl"),
                                           ("cst", "s_cst"), ("ptw", "s_ptw"), ("carry", None), ("ost0", "s_ost0"), ("ost1", "s_ost1"),
                                           ("zero", None), ("ones", None), ("onesf", None), ("warm", None), ("sqtop", None), ("stp", "s_stp"), ("stc", "s_stc"), ("sts", "s_sts"),
                                           ("stf", "s_stf"), ("rstd", None)]}
            psb = [Buf(f"ps{i}") for i in range(4)]
            xnB = [Buf(f"xn{i}") for i in range(8)]
            hB = [Buf(f"h{i}", "s_h") for i in range(KC)]
            sqB = [Buf(f"sq{i}") for i in range(4)]
            st = {"psj": 0, "fresh_xn": False}

            sqtop_v = U_t[:, NU - KC * WMAX // 2:NU].bitcast(BF16).rearrange("p (k w) -> p k w", k=KC)
            yv0 = xn_t[:].rearrange("p k w -> p (k w)").bitcast(F32).rearrange("p (k w) -> p k w", k=8)
            yv1 = cat_t[:].rearrange("p k w -> p (k w)").bitcast(F32).rearrange("p (k w) -> p k w", k=8)

            def cs(name, i0, n=1):
                o = COFF[name] + i0
                return cst_t[:, o:o + n]

            S.dma("sp", lambda e: e.dma_start(out=cst_t[:], in_=cst_in), "s_cst", writes=[B["cst"]])
            S.op("dve", lambda e: e.memset(zero_t[:], 0.0), writes=[B["zero"]])
            S.op("dve", lambda e: e.memset(ones_t[:], 1.0), writes=[B["ones"]])
            S.op("dve", lambda e: e.memset(ones_f[:], 1.0), writes=[B["onesf"]])

            for pi in range(2):
                if pi == 0:
                    S.op("act", (lambda e: e.activation(out=warm_t[:, 0:1], in_=ones_f[:, 0:1], func=AF.Ln)), reads=[B["onesf"]], writes=[B["warm"]])
                Ws = NS if pi == 1 else 0
                W = WP + Ws
                Wh = W // 2
                c0 = pi * WP

                def v2(ap):
                    return ap.rearrange("p (a b) -> p a b", a=2)

                def psv(slot):
                    return ps_t[:, 2 * slot:2 * slot + 2, 0:Wh]

                def mm_job(nk, lhs, rhs, reads, kreads=None):
                    slot = st["psj"] % 4
                    st["psj"] += 1
                    groups = kreads if kreads else [(0, nk, [])]
                    for (k0, k1, extra_reads) in groups:
                        def fn(e, slot=slot, nk=nk, lhs=lhs, rhs=rhs, k0=k0, k1=k1):
                            for k in range(k0, k1):
                                for t in range(2):
                                    e.matmul(ps_t[:, 2 * slot + t, 0:Wh], lhsT=lhs(k), rhs=rhs(k)[:, t * Wh:(t + 1) * Wh],
                                             start=(k == 0), stop=(k == nk - 1))
                        S.op("pe", fn, reads=list(reads) + list(extra_reads), writes=[psb[slot]])
                    return slot

                def linear(mat, t, nk, rhs_t, rhs_buf):
                    wbuf, wv = ring.next(mat, t, 0, nk)
                    if rhs_buf == "xn":
                        if st["fresh_xn"]:
                            st["fresh_xn"] = False
                            return mm_job(nk, (lambda k, wv=wv: wv[:, k, :]), (lambda k, rhs_t=rhs_t: rhs_t[:, k, 0:W]), [wbuf],
                                          kreads=[(2 * i, 2 * i + 2, [xnB[i]]) for i in range(8)])
                        return mm_job(nk, (lambda k, wv=wv: wv[:, k, :]), (lambda k, rhs_t=rhs_t: rhs_t[:, k, 0:W]), [wbuf] + xnB)
                    return mm_job(nk, (lambda k, wv=wv: wv[:, k, :]), (lambda k, rhs_t=rhs_t: rhs_t[:, k, 0:W]), [wbuf, rhs_buf])

                def linear_multi(specs):
                    st["fresh_xn"] = False
                    ring.hold = len(specs) - 1
                    jobs = []
                    for (mat, t) in specs:
                        wbuf, wv = ring.next(mat, t, 0, KC)
                        slot = st["psj"] % 4
                        st["psj"] += 1
                        jobs.append((wbuf, wv, slot))
                    for g in range(8):
                        for (wbuf, wv, slot) in jobs:
                            def fn(e, wv=wv, slot=slot, g=g):
                                for k in (2 * g, 2 * g + 1):
                                    for t in range(2):
                                        e.matmul(ps_t[:, 2 * slot + t, 0:Wh], lhsT=wv[:, k, :], rhs=xn_t[:, k, t * Wh:(t + 1) * Wh],
                                                 start=(k == 0), stop=(k == KC - 1))
                            S.op("pe", fn, reads=[wbuf, xnB[g]], writes=[psb[slot]])
                    ring.hold = 0
                    return [j[2] for j in jobs]

                def sq_region(region):
                    if region == "xn":
                        return xn_t, (lambda k: [xnB[k // 2]])
                    if region == "top":
                        return sqtop_v, (lambda k: [B["sqtop"]])
                    return cat_t, (lambda k: [B["cat"]])

                def xnp_chunk(m, gname, l):
                    g = cs(gname, l * 16 + m)
                    S.op("act", (lambda e: e.activation(out=xn_t[:, m, 0:W], in_=h_t[:, m, 0:W], func=AF.Identity, scale=g)),
                         reads=[hB[m], B["cst"]], writes=[xnB[m // 2]])

                def rmsnorm_stats(region):
                    sqv, bf = sq_region(region)
                    early = []
                    for k in range(0, 14, 2):
                        early += bf(k)
                    slot = mm_job(KC, (lambda k: ones_t[:]), (lambda k: sqv[:, k, 0:W]), [B["ones"]], kreads=[(0, 14, early), (14, 16, bf(14))])
                    S.op("act", (lambda e: e.activation(out=v2(rstd_t[:, 0:W]), in_=psv(slot), func=AF.Ln, scale=1.0 / D, bias=EPS)),
                         reads=[psb[slot]], writes=[B["rstd"]])
                    S.op("act", (lambda e: e.activation(out=rstd_t[:, 0:W], in_=rstd_t[:, 0:W], func=AF.Exp, scale=-0.5)), reads=[B["rstd"]], writes=[B["rstd"]])

                def warm_lnexp():
                    S.op("act", (lambda e: e.activation(out=warm_t[:, 0:1], in_=ones_f[:, 0:1], func=AF.Ln)), reads=[B["onesf"]], writes=[B["warm"]])

                def sq_chunk(m, region):
                    sqv, bf = sq_region(region)
                    S.op("act", (lambda e: e.activation(out=sqv[:, m, 0:W], in_=h_t[:, m, 0:W], func=AF.Square)), reads=[hB[m]], writes=bf(m))

                def rmsnorm(gname, l, region, presq, final=False):
                    sqv, bf = sq_region(region)
                    if not presq:
                        for q in range(4):
                            S.op("act", (lambda e, q=q: e.activation(out=sqv[:, 4 * q:4 * q + 4, 0:W], in_=h_t[:, 4 * q:4 * q + 4, 0:W], func=AF.Square)),
                                 reads=hB[4 * q:4 * q + 4], writes=bf(4 * q) + bf(4 * q + 2))
                    if presq:
                        early = []
                        for k in range(0, 14, 2):
                            early += bf(k)
                        kr = [(0, 14, early), (14, 16, bf(14))]
                    else:
                        kr = [(4 * i, 4 * i + 4, bf(4 * i) + bf(4 * i + 2)) for i in range(4)]
                    slot = mm_job(KC, (lambda k: ones_t[:]), (lambda k: sqv[:, k, 0:W]), [B["ones"]], kreads=kr)
                    S.op("act", (lambda e, slot=slot: e.activation(out=v2(rstd_t[:, 0:W]), in_=psv(slot), func=AF.Ln, scale=1.0 / D, bias=EPS)),
                         reads=[psb[slot]], writes=[B["rstd"]])
                    S.op("act", (lambda e: e.activation(out=rstd_t[:, 0:W], in_=rstd_t[:, 0:W], func=AF.Exp, scale=-0.5)), reads=[B["rstd"]], writes=[B["rstd"]])
                    for k in range(KC):
                        g = cs(gname, (0 if final else l * 16) + k)
                        if final:
                            yv = yv0[:, k, 0:W] if k < 8 else yv1[:, k - 8, 0:W]
                            S.op("dve", (lambda e, k=k, g=g, yv=yv: e.scalar_tensor_tensor(out=yv, in0=h_t[:, k, 0:W], scalar=g, in1=rstd_t[:, 0:W],
                                                                                            op0=ALU.mult, op1=ALU.mult)),
                                 reads=[hB[k], B["rstd"], B["cst"]], writes=([xnB[k]] if k < 8 else [B["cat"]]))
                        else:
                            S.op("dve", (lambda e, k=k, g=g: e.scalar_tensor_tensor(out=xn_t[:, k, 0:W], in0=h_t[:, k, 0:W], scalar=g, in1=rstd_t[:, 0:W],
                                                                                     op0=ALU.mult, op1=ALU.mult)),
                                 reads=[hB[k], B["rstd"], B["cst"]], writes=[xnB[k // 2]])
                    if not final:
                        st["fresh_xn"] = True

                if Ws:
                    S.dma("sp", (lambda e: e.dma_start(out=h_t[:, :, WP:W], in_=xT[:, :, TPC:TC])), "s_h", writes=hB)
                for q in range(4):
                    S.dma("sp", (lambda e, q=q: e.dma_start(out=h_t[:, 4 * q:4 * q + 4, 0:WP], in_=xT[:, 4 * q:4 * q + 4, c0:c0 + WP])), f"s_hx{q}", writes=hB[4 * q:4 * q + 4])

                if pi == 0:
                    S.wait("pool", [("s_hx2", 16)])

                for l in range(L):
                    def hist(off, n, l=l):
                        if pi == 0:
                            return zero_t[:, 0:n]
                        return carry_t[:, l, off:off + n]
                    hbuf = B["zero"] if pi == 0 else B["carry"]

                    def savedst(off, n, l=l):
                        return (carry_t if pi == 0 else ost_t)[:, l, off:off + n]
                    sbuf_ = B["carry"] if pi == 0 else B[f"ost{l}"]

                    S.dma("pool", (lambda e, l=l, c0=c0: e.dma_start(out=ptb_t[:, :, 0:WP], in_=pT[l, :, :, c0:c0 + WP])), "s_ptb", writes=[B["ptb"]])
                    if Ws:
                        S.dma("pool", (lambda e, l=l: e.dma_start(out=ptb_t[:, :, WP:W], in_=pT[l, :, :, TPC:TC])), "s_ptb", writes=[B["ptb"]])
                    S.dma("pool", (lambda e, l=l: e.dma_start(out=wpl_t[:], in_=w_pool[l * 512:(l + 1) * 512, :].rearrange("(g c) d -> c g d", c=128))),
                          "s_wpl", writes=[B["wpl"]])
                    if Ws:
                        S.dma("sp", (lambda e, l=l: e.dma_start(out=stp_t[:].rearrange("p a b c -> p (a b c)"), in_=hp_in[l])), "s_stp", writes=[B["stp"]])
                        S.dma("sp", (lambda e, l=l: e.dma_start(out=stc_t[:].rearrange("p a b c -> p (a b c)"), in_=hc_in[l])), "s_stc", writes=[B["stc"]])
                        S.dma("sp", (lambda e, l=l: e.dma_start(out=sts_t[:].rearrange("p a b c -> p (a b c)"), in_=hs_in[l])), "s_sts", writes=[B["sts"]])
                        S.dma("sp", (lambda e, l=l: e.dma_start(out=stf_t[:].rearrange("p a b c d -> p (a b c d)"), in_=hf_in[l])), "s_stf", writes=[B["stf"]])

                    S.barrier(("act", "dve"))
                    rmsnorm("nmix", l, ("cat" if l > 0 else "xn"), presq=(l > 0))
                    ar = Arena()
                    sig_a = [ar.f32(WMAX) for _ in range(2)]
                    G_a = [ar.f32(30 + WMAX) for _ in range(2)]
                    Gb_a = [ar.bf16(30 + WMAX) for _ in range(2)]
                    D_a = ar.bf16(31 * 128).rearrange("p (k m) -> p k m", k=31)
                    CB_a = ar.f32(6 * WMAX).rearrange("p (j w) -> p j w", j=6)
                    sil_a = ar.bf16(6 * WMAX).rearrange("p (j w) -> p j w", j=6)
                    tmpc_a = ar.f32(16 * 29).rearrange("p (b s) -> p b s", b=16)
                    r1_a = ar.f32(16)
                    hb_a = [ar.f32(WMAX) for _ in range(2)]
                    V_a = [ar.f32(2 + WMAX) for _ in range(2)]
                    cv_a = [ar.f32(WMAX) for _ in range(2)]
                    U_a = [ar.f32(15 + WMAX) for _ in range(2)]
                    S_a = [ar.f32(15 + WMAX) for _ in range(2)]
                    dbf_a = [ar.bf16(WMAX) for _ in range(2)]
                    t16_a = ar.f32(16)
                    sigB = [Buf("sig0"), Buf("sig1")]
                    GB = [Buf("G0"), Buf("G1")]
                    GbB = [Buf("Gb0"), Buf("Gb1")]
                    DB = Buf("D")
                    CBBj = [Buf(f"CB{j}") for j in range(6)]
                    silB = Buf("sil")
                    tmpB = Buf("tmpc")
                    hbB = [Buf("hb0"), Buf("hb1")]
                    VB = [Buf("V0"), Buf("V1")]
                    cvB = [Buf("cv0"), Buf("cv1")]
                    UB = [Buf("U0"), Buf("U1")]
                    SB_ = [Buf("S0"), Buf("S1")]
                    dB = [Buf("d0"), Buf("d1")]
                    ident_v = cs("ident", 0, 128)

                    prez = dict(zip((4, 10, 5), linear_multi([(w_in, l * 34 + 4), (w_in, l * 34 + 10), (w_in, l * 34 + 5)])))

                    def zin(m, l=l):
                        if m in prez:
                            return prez.pop(m)
                        return linear(w_in, l * 34 + m, KC, xn_t, "xn")

                    def conf_pair(j):
                        i2 = j % 2
                        sa_ = zin(4 + j)
                        sb_ = zin(10 + j)
                        G, sg, Gb = G_a[i2], sig_a[i2], Gb_a[i2]
                        S.op("act", (lambda e: e.activation(out=v2(sg[:, 0:W]), in_=psv(sb_), func=AF.Sigmoid)),
                             reads=[psb[sb_]], writes=[sigB[i2]])
                        S.op("dve", (lambda e: e.tensor_copy(out=G[:, 0:30], in_=hist(OFF_C + 30 * j, 30))), reads=[hbuf], writes=[GB[i2]])
                        S.op("dve", (lambda e: e.tensor_tensor(out=v2(G[:, 30:30 + W]), in0=psv(sa_), in1=v2(sg[:, 0:W]), op=ALU.mult)),
                             reads=[psb[sa_], sigB[i2]], writes=[GB[i2]])
                        S.op("act", (lambda e: e.activation(out=Gb[:, 0:30 + W], in_=G[:, 0:30 + W], func=AF.Copy)), reads=[GB[i2]], writes=[GbB[i2]])
                        S.op("dve", (lambda e: e.tensor_copy(out=savedst(OFF_C + 30 * j, 30), in_=G[:, WP:WP + 30])), reads=[GB[i2]], writes=[sbuf_])
                        if Ws:
                            S.op("dve", (lambda e: e.tensor_copy(out=stc_t[:, j, 29, :], in_=G[:, 30 + WP:30 + W])), reads=[GB[i2]], writes=[B["stc"]])

                    def conf_dgen(j):
                        wo = COFF["cdw"] + (l * 6 + j) * 31
                        wbc = cst_t[:, wo + NDT:wo + 31].unsqueeze(2).broadcast_to([128, 31 - NDT, 128])
                        idb = ident_v.unsqueeze(1).broadcast_to([128, 31 - NDT, 128])
                        S.op("dve", (lambda e: e.tensor_tensor(out=D_a[:, NDT:31, :], in0=idb, in1=wbc, op=ALU.mult)), reads=[B["cst"]], writes=[DB])

                    def conf_conv(j):
                        i2 = j % 2
                        Gb = Gb_a[i2]
                        G = G_a[i2]
                        wo = COFF["cdw"] + (l * 6 + j) * 31
                        S.op("dve", (lambda e: e.tensor_scalar(out=CB_a[:, j, 0:WP], in0=G[:, 0:WP], scalar1=cst_t[:, wo:wo + 1], scalar2=None, op0=ALU.mult)),
                             reads=[GB[i2], B["cst"]], writes=[CBBj[j]])
                        for k in range(1, NDT):
                            S.op("dve", (lambda e, k=k: e.scalar_tensor_tensor(out=CB_a[:, j, 0:WP], in0=G[:, k:k + WP], scalar=cst_t[:, wo + k:wo + k + 1],
                                                                                in1=CB_a[:, j, 0:WP], op0=ALU.mult, op1=ALU.add)),
                                 reads=[GB[i2], B["cst"]], writes=[CBBj[j]])
                        slot = st["psj"] % 4
                        st["psj"] += 1

                        def fn(e):
                            order = [30] + list(range(NDT, 30))
                            for n_, k in enumerate(order):
                                for t in range(2):
                                    lo = t * Wh
                                    hi = (t + 1) * Wh if k == 30 else min((t + 1) * Wh, WP)
                                    e.matmul(ps_t[:, 2 * slot + t, 0:hi - lo], lhsT=D_a[:, k, :], rhs=Gb[:, k + lo:k + hi],
                                             start=(n_ == 0), stop=(n_ == len(order) - 1))
                        S.op("pe", fn, reads=[DB, GbB[i2]], writes=[psb[slot]])
                        bias_ = cs("cdb", l * 6 + j)
                        if Ws:
                            S.op("dve", (lambda e: e.scalar_tensor_tensor(out=CB_a[:, j, 0:Wh], in0=ps_t[:, 2 * slot, 0:Wh], scalar=bias_, in1=CB_a[:, j, 0:Wh],
                                                                          op0=ALU.add, op1=ALU.add)), reads=[psb[slot], B["cst"]], writes=[CBBj[j]])
                            S.op("dve", (lambda e: e.scalar_tensor_tensor(out=CB_a[:, j, Wh:WP], in0=ps_t[:, 2 * slot + 1, 0:WP - Wh], scalar=bias_, in1=CB_a[:, j, Wh:WP],
                                                                          op0=ALU.add, op1=ALU.add)), reads=[psb[slot], B["cst"]], writes=[CBBj[j]])
                            S.op("act", (lambda e: e.activation(out=CB_a[:, j, WP:W], in_=ps_t[:, 2 * slot + 1, WP - Wh:Wh], func=AF.Identity, bias=bias_)),
                                 reads=[psb[slot], B["cst"]], writes=[CBBj[j]])
                        else:
                            S.op("dve", (lambda e: e.scalar_tensor_tensor(out=v2(CB_a[:, j, 0:W]), in0=psv(slot), scalar=bias_, in1=v2(CB_a[:, j, 0:W]),
                                                                          op0=ALU.add, op1=ALU.add)), reads=[psb[slot], B["cst"]], writes=[CBBj[j]])
                        if Ws:
                            wb2 = cst_t[:, wo + 1:wo + 30].unsqueeze(1).broadcast_to([128, 16, 29])
                            S.op("dve", (lambda e: e.tensor_tensor(out=tmpc_a, in0=stc_t[:, j, 0:29, :].rearrange("p s b -> p b s"), in1=wb2, op=ALU.mult)),
                                 reads=[B["stc"], B["cst"]], writes=[tmpB])
                            S.op("dve", (lambda e: e.tensor_reduce(out=r1_a, in_=tmpc_a, axis=AX.X, op=ALU.add)), reads=[tmpB], writes=[tmpB])
                            S.op("dve", (lambda e: e.scalar_tensor_tensor(out=r1_a, in0=stc_t[:, j, 30, :], scalar=cst_t[:, wo:wo + 1], in1=r1_a,
                                                                          op0=ALU.mult, op1=ALU.add)), reads=[B["stc"], B["cst"], tmpB], writes=[tmpB])
                            S.op("dve", (lambda e: e.tensor_tensor(out=CB_a[:, j, WP:W], in0=CB_a[:, j, WP:W], in1=r1_a, op=ALU.add)), reads=[tmpB], writes=[CBBj[j]])

                    lnr = sig_a[0]

                    def ln_prep():
                        S.op("act", (lambda e: e.activation(out=sil_a[:, :, 0:W], in_=CB_a[:, :, 0:W], func=AF.Copy)), reads=CBBj, writes=[silB])

                    def ln_mean():
                        slot = mm_job(6, (lambda k: ones_t[:]), (lambda k: sil_a[:, k, 0:W]), [B["ones"], silB])
                        for j in range(6):
                            S.op("dve", (lambda e, j=j: e.scalar_tensor_tensor(out=v2(CB_a[:, j, 0:W]), in0=psv(slot), scalar=-1.0 / 768, in1=v2(CB_a[:, j, 0:W]),
                                                                                op0=ALU.mult, op1=ALU.add)), reads=[psb[slot]], writes=[CBBj[j]])
                        S.op("act", (lambda e: e.activation(out=sil_a[:, :, 0:W], in_=CB_a[:, :, 0:W], func=AF.Square)), reads=CBBj, writes=[silB])

                    def ln_var():
                        slot = mm_job(6, (lambda k: ones_t[:]), (lambda k: sil_a[:, k, 0:W]), [B["ones"], silB])
                        S.op("act", (lambda e: e.activation(out=v2(lnr[:, 0:W]), in_=psv(slot), func=AF.Ln, scale=1.0 / 768, bias=EPS)),
                             reads=[psb[slot]], writes=[sigB[0]])
                        S.op("act", (lambda e: e.activation(out=lnr[:, 0:W], in_=lnr[:, 0:W], func=AF.Exp, scale=-0.5)), reads=[sigB[0]], writes=[sigB[0]])
                        for j in range(6):
                            S.op("dve", (lambda e, j=j: e.tensor_tensor(out=CB_a[:, j, 0:W], in0=CB_a[:, j, 0:W], in1=lnr[:, 0:W], op=ALU.mult)), reads=[sigB[0]], writes=[CBBj[j]])
                            S.op("act", (lambda e, j=j: e.activation(out=sil_a[:, j, 0:W], in_=CB_a[:, j, 0:W], func=AF.Silu, scale=cs("lng", l * 6 + j), bias=cs("lnb", l * 6 + j))),
                                 reads=[CBBj[j], B["cst"]], writes=[silB])

                    def conf_pw_jobs():
                        for m in range(6):
                            wbuf, wv = ring.next(conf_pw, l * 6 + m, 0, 6)
                            slot = mm_job(6, (lambda k, wv=wv: wv[:, k, :]), (lambda k: sil_a[:, k, 0:W]), [wbuf, silB])
                            S.op("act", (lambda e, m=m, slot=slot: e.activation(out=v2(cat_t[:, 4 + m, 0:W]), in_=psv(slot), func=AF.Identity, bias=cs("pwb", l * 6 + m))),
                                 reads=[psb[slot], B["cst"]], writes=[B["cat"]])

                    def sc_triple(j):
                        i2 = j % 2
                        sc_ = zin(22 + j)
                        sh_ = zin(28 + j)
                        sbb = zin(16 + j)
                        hb, V, cv = hb_a[i2], V_a[i2], cv_a[i2]
                        wo = COFF["scw"] + (l * 6 + j) * 3
                        S.op("act", (lambda e: e.activation(out=v2(hb[:, 0:W]), in_=psv(sh_), func=AF.Copy)), reads=[psb[sh_]], writes=[hbB[i2]])
                        S.op("dve", (lambda e: e.tensor_copy(out=V[:, 0:2], in_=hist(OFF_S + 2 * j, 2))), reads=[hbuf], writes=[VB[i2]])
                        S.op("dve", (lambda e: e.tensor_tensor(out=v2(V[:, 2:2 + W]), in0=psv(sc_), in1=v2(hb[:, 0:W]), op=ALU.mult)),
                             reads=[psb[sc_], hbB[i2]], writes=[VB[i2]])
                        S.op("dve", (lambda e: e.tensor_copy(out=savedst(OFF_S + 2 * j, 2), in_=V[:, WP:WP + 2])), reads=[VB[i2]], writes=[sbuf_])
                        S.op("act", (lambda e: e.activation(out=cv[:, 0:W], in_=V[:, 2:2 + W], func=AF.Identity, scale=cst_t[:, wo + 2:wo + 3])),
                             reads=[VB[i2], B["cst"]], writes=[cvB[i2]])
                        for k in range(2):
                            S.op("dve", (lambda e, k=k: e.scalar_tensor_tensor(out=cv[:, 0:WP], in0=V[:, k:k + WP], scalar=cst_t[:, wo + k:wo + k + 1],
                                                                                in1=cv[:, 0:WP], op0=ALU.mult, op1=ALU.add)),
                                 reads=[VB[i2], B["cst"]], writes=[cvB[i2]])
                        if Ws:
                            S.op("dve", (lambda e: e.tensor_copy(out=sts_t[:, j, 1, :], in_=V[:, 2 + WP:2 + W])), reads=[VB[i2]], writes=[B["sts"]])
                            for (slot_k, k) in ((0, 1), (2, 0)):
                                S.op("dve", (lambda e, slot_k=slot_k, k=k: e.scalar_tensor_tensor(out=cv[:, WP:W], in0=sts_t[:, j, slot_k, :],
                                                                                                   scalar=cst_t[:, wo + k:wo + k + 1], in1=cv[:, WP:W],
                                                                                                   op0=ALU.mult, op1=ALU.add)),
                                     reads=[B["sts"], B["cst"]], writes=[cvB[i2]])
                        S.op("dve", (lambda e: e.tensor_tensor(out=v2(cat_t[:, 10 + j, 0:W]), in0=psv(sbb), in1=v2(cv[:, 0:W]), op=ALU.mult)),
                             reads=[psb[sbb], cvB[i2]], writes=[B["cat"]])

                    def pool_in(g):
                        i2 = g % 2
                        w = POOLW[g]
                        su = zin(g)
                        Ub, S1, S2, dbf = U_a[i2], S_a[0], S_a[1], dbf_a[i2]
                        S.op("dve", (lambda e: e.tensor_copy(out=Ub[:, 0:15], in_=hist(OFF_P + 15 * g, 15))), reads=[hbuf], writes=[UB[i2]])
                        S.op("act", (lambda e: e.activation(out=v2(Ub[:, 15:15 + W]), in_=psv(su), func=AF.Copy)), reads=[psb[su]], writes=[UB[i2]])
                        S.op("dve", (lambda e: e.tensor_copy(out=savedst(OFF_P + 15 * g, 15), in_=Ub[:, WP:WP + 15])), reads=[UB[i2]], writes=[sbuf_])
                        src, srcB = Ub, UB[i2]
                        E = 15 + WP
                        for i in range(g + 1):
                            dst, dstB = (S1, SB_[0]) if i % 2 == 0 else (S2, SB_[1])
                            sh = 1 << i
                            lo = 2 * sh - 1
                            S.op("dve", (lambda e, src=src, dst=dst, sh=sh, lo=lo: e.tensor_tensor(out=dst[:, lo:E], in0=src[:, lo:E], in1=src[:, lo - sh:E - sh], op=ALU.add)),
                                 reads=[srcB], writes=[dstB])
                            src, srcB = dst, dstB
                        S.op("dve", (lambda e: e.scalar_tensor_tensor(out=dbf[:, 0:WP], in0=src[:, 15:15 + WP], scalar=1.0 / w, in1=Ub[:, 15:15 + WP],
                                                                      op0=ALU.mult, op1=ALU.subtract)),
                             reads=[srcB, UB[i2]], writes=[dB[i2]])
                        if pi == 0:
                            S.op("dve", (lambda e: e.tensor_tensor(out=t16_a, in0=src[:, 15:31], in1=cs("invc", 16 * g, 16), op=ALU.mult)),
                                 reads=[srcB, B["cst"]], writes=[tmpB])
                            S.op("dve", (lambda e: e.tensor_tensor(out=dbf[:, 0:16], in0=t16_a, in1=Ub[:, 15:31], op=ALU.subtract)),
                                 reads=[tmpB, UB[i2]], writes=[dB[i2]])
                        if Ws:
                            S.op("dve", (lambda e: e.tensor_copy(out=stp_t[:, g, 14, :], in_=Ub[:, 15 + WP:15 + W])), reads=[UB[i2]], writes=[B["stp"]])
                            s0 = 0 if w == 16 else 15 - w
                            s1 = 16 if w == 16 else 15
                            S.op("dve", (lambda e: e.tensor_reduce(out=t16_a, in_=stp_t[:, g, s0:s1, :].rearrange("p s b -> p b s"), axis=AX.X, op=ALU.add)),
                                 reads=[B["stp"]], writes=[tmpB])
                            S.op("dve", (lambda e: e.scalar_tensor_tensor(out=dbf[:, WP:W], in0=t16_a, scalar=1.0 / w, in1=Ub[:, 15 + WP:15 + W],
                                                                          op0=ALU.mult, op1=ALU.subtract)),
                                 reads=[tmpB, UB[i2]], writes=[dB[i2]])

                    def pool_mm(g):
                        i2 = g % 2
                        dbf = dbf_a[i2]
                        slot = mm_job(1, (lambda k: wpl_t[:, g, :]), (lambda k: dbf[:, 0:W]), [B["wpl"], dB[i2]])
                        S.op("act", (lambda e: e.activation(out=v2(cat_t[:, g, 0:W]), in_=psv(slot), func=AF.Identity, scale=cs("pscale", l * 4 + g))),
                             reads=[psb[slot], B["cst"]], writes=[B["cat"]])

                    conf_dgen(0)
                    conf_pair(0)
                    for j in range(1, 6):
                        conf_pair(j)
                        conf_conv(j - 1)
                        conf_dgen(j)
                    sc_triple(0)
                    conf_conv(5)
                    ln_prep()
                    sc_triple(1)
                    ln_mean()
                    sc_triple(2)
                    ln_var()
                    sc_triple(3)
                    sc_triple(4)
                    sc_triple(5)
                    pool_in(0)
                    pool_in(1)
                    pool_mm(0)
                    pool_in(2)
                    pool_mm(1)
                    pool_in(3)
                    pool_mm(2)
                    conf_pw_jobs()
                    pool_mm(3)

                    warm_lnexp()
                    for m in range(KC):
                        slot = linear(w_out, l * KC + m, KC, cat_t, B["cat"])
                        S.op("dve", (lambda e, m=m, slot=slot: e.tensor_tensor(out=v2(h_t[:, m, 0:W]), in0=psv(slot), in1=v2(h_t[:, m, 0:W]), op=ALU.add)),
                             reads=[psb[slot]], writes=[hB[m]])
                        sq_chunk(m, "xn")

                    S.barrier(("act", "dve"))
                    btok = [(o, S.cnt[o]) for o in ("pe", "act", "dve")]
                    rmsnorm("nffn", l, "xn", presq=True)
                    ar = Arena()
                    ptw_a = ar.bf16(2 * D).rearrange("p (k m) -> p k m", k=2)
                    raw_a = [ar.f32(2 * (2 + WMAX)).rearrange("p (a w) -> p a w", a=2) for _ in range(2)]
                    acc_a = [ar.f32(2 * WMAX).rearrange("p (a w) -> p a w", a=2) for _ in range(2)]
                    sa_a = [ar.f32(WMAX) for _ in range(2)]
                    act_a = [ar.bf16(GS * WMAX).rearrange("p (j w) -> p j w", j=GS) for _ in range(2)]
                    rawB = [[Buf(f"raw{i}{ab}") for ab in range(2)] for i in range(2)]
                    accB = [[Buf(f"acc{i}{ab}") for ab in range(2)] for i in range(2)]
                    saB = [Buf("sa0"), Buf("sa1")]
                    actB = [Buf("act0"), Buf("act1")]
                    S.dma("pool", (lambda e: e.dma_start(out=ptw_a, in_=ple_proj[l * 256:(l + 1) * 256, :].rearrange("(k p) m -> p k m", p=128))),
                          "s_ptw", writes=[B["ptw"]], extra=btok)

                    def down_proj(q):
                        qa = q % 2
                        for m in range(KC):
                            wbuf, wv = ring.next(w_down, l * KC + m, q * GS, GS)
                            slot = mm_job(GS, (lambda k, wv=wv: wv[:, k, :]), (lambda k, qa=qa: act_a[qa][:, k, 0:W]), [wbuf, actB[qa]])
                            S.op("dve", (lambda e, m=m, slot=slot: e.tensor_tensor(out=v2(h_t[:, m, 0:W]), in0=psv(slot), in1=v2(h_t[:, m, 0:W]), op=ALU.add)),
                                 reads=[psb[slot]], writes=[hB[m]])
                            if q == FC // GS - 1:
                                xnp_chunk(m, "nple", l)
                                sq_chunk(m, "cat")

                    def ffn_tail(j):
                        i2, qa, jj = j % 2, (j // GS) % 2, j % GS
                        acc, sa = acc_a[i2], sa_a[i2]
                        S.op("act", (lambda e: e.activation(out=sa[:, 0:W], in_=acc[:, 0, 0:W], func=AF.Silu)), reads=[accB[i2][0]], writes=[saB[i2]])
                        S.op("dve", (lambda e: e.tensor_tensor(out=act_a[qa][:, jj, 0:W], in0=sa[:, 0:W], in1=acc[:, 1, 0:W], op=ALU.mult)),
                             reads=[saB[i2], accB[i2][1]], writes=[actB[qa]])

                    preu = dict(zip(((0, 0), (0, 1), (1, 0)),
                                    linear_multi([(w_up, l * 2 * FC + 0), (w_up, l * 2 * FC + FC), (w_up, l * 2 * FC + 1)])))
                    for q in range(FC // GS):
                        for jj in range(GS):
                            if q > 0 and jj == 2 and q < FC // GS - 1:
                                down_proj(q - 1)
                            j = q * GS + jj
                            i2 = j % 2
                            raw, acc = raw_a[i2], acc_a[i2]
                            wo = COFF["ffw"] + (l * FC + j) * 6
                            S.op("dve", (lambda e: e.tensor_copy(out=raw[:, :, 0:2], in_=hist(OFF_F + 4 * j, 4).rearrange("p (a k) -> p a k", a=2))),
                                 reads=[hbuf], writes=rawB[i2])
                            for ab in range(2):
                                s_ = preu.pop((j, ab)) if (j, ab) in preu else linear(w_up, l * 2 * FC + ab * FC + j, KC, xn_t, "xn")
                                S.op("act", (lambda e: e.activation(out=v2(raw[:, ab, 2:2 + W]), in_=psv(s_), func=AF.Copy)),
                                     reads=[psb[s_]], writes=[rawB[i2][ab]])
                                S.op("act", (lambda e: e.activation(out=v2(acc[:, ab, 0:W]), in_=psv(s_), func=AF.Identity,
                                                                    scale=cst_t[:, wo + 3 * ab + 2:wo + 3 * ab + 3])),
                                     reads=[psb[s_], B["cst"]], writes=[accB[i2][ab]])
                                for k in range(2):
                                    S.op("dve", (lambda e, k=k: e.scalar_tensor_tensor(out=acc[:, ab, 0:WP], in0=raw[:, ab, k:k + WP],
                                                                                        scalar=cst_t[:, wo + 3 * ab + k:wo + 3 * ab + k + 1],
                                                                                        in1=acc[:, ab, 0:WP], op0=ALU.mult, op1=ALU.add)),
                                         reads=[rawB[i2][ab], B["cst"]], writes=[accB[i2][ab]])
                                if Ws:
                                    for (slot_k, k) in ((0, 1), (2, 0)):
                                        S.op("dve", (lambda e, slot_k=slot_k, k=k: e.scalar_tensor_tensor(
                                            out=acc[:, ab, WP:W], in0=stf_t[:, j, ab, slot_k, :], scalar=cst_t[:, wo + 3 * ab + k:wo + 3 * ab + k + 1],
                                            in1=acc[:, ab, WP:W], op0=ALU.mult, op1=ALU.add)),
                                            reads=[B["stf"], B["cst"]], writes=[accB[i2][ab]])
                            S.op("dve", (lambda e: e.tensor_copy(out=savedst(OFF_F + 4 * j, 4).rearrange("p (a k) -> p a k", a=2), in_=raw[:, :, WP:WP + 2])),
                                 reads=rawB[i2], writes=[sbuf_])
                            if Ws:
                                S.op("dve", (lambda e: e.tensor_copy(out=stf_t[:, j, :, 1, :], in_=raw[:, :, 2 + WP:2 + W])), reads=rawB[i2], writes=[B["stf"]])
                            if j > 0:
                                ffn_tail(j - 1)
                    ffn_tail(FC - 1)
                    down_proj(FC // GS - 2)
                    warm_lnexp()
                    down_proj(FC // GS - 1)

                    S.barrier(("act", "dve"))
                    st["fresh_xn"] = True
                    ar = Arena()
                    ptw_a = ar.bf16(2 * D).rearrange("p (k m) -> p k m", k=2)
                    gs_a = [ar.f32(WMAX) for _ in range(2)]
                    pg_a = [ar.f32(WMAX) for _ in range(2)]
                    gsB = [Buf("gs0"), Buf("gs1")]
                    pgB = [Buf("pg0"), Buf("pg1")]
                    for m in range(KC):
                        i2 = m % 2
                        sg_ = linear(ple_gate, l * KC + m, KC, xn_t, "xn")
                        if m == 0:
                            rmsnorm_stats("cat")
                        S.op("dve", (lambda e, sg_=sg_, i2=i2: e.tensor_tensor(out=v2(gs_a[i2][:, 0:W]), in0=psv(sg_), in1=v2(rstd_t[:, 0:W]), op=ALU.mult)),
                             reads=[psb[sg_], B["rstd"]], writes=[gsB[i2]])
                        S.op("act", (lambda e, i2=i2: e.activation(out=gs_a[i2][:, 0:W], in_=gs_a[i2][:, 0:W], func=AF.Sigmoid)), reads=[gsB[i2]], writes=[gsB[i2]])
                        if m == KC - 1:
                            warm_lnexp()
                        sp_ = mm_job(2, (lambda k, m=m: ptw_a[:, k, m * 128:(m + 1) * 128]), (lambda k: ptb_t[:, k, 0:W]), [B["ptw"], B["ptb"]])
                        S.op("dve", (lambda e, sp_=sp_, i2=i2: e.tensor_tensor(out=v2(pg_a[i2][:, 0:W]), in0=psv(sp_), in1=v2(gs_a[i2][:, 0:W]), op=ALU.mult)),
                             reads=[psb[sp_], gsB[i2]], writes=[pgB[i2]])
                        S.op("dve", (lambda e, m=m, i2=i2: e.tensor_tensor(out=h_t[:, m, 0:W], in0=h_t[:, m, 0:W], in1=pg_a[i2][:, 0:W], op=ALU.add)),
                             reads=[pgB[i2]], writes=[hB[m]])
                        sq_chunk(m, "cat")
                    S.barrier(("act", "dve"))

                    if pi == 1:
                        S.dma("sp", (lambda e, l=l: e.dma_start(out=pst_out[l], in_=ost_t[:, l, :])), f"s_ost{l}", reads=[B[f"ost{l}"]])
                        S.dma("sp", (lambda e, l=l: e.dma_start(out=op_out[l], in_=stp_t[:].rearrange("p a b c -> p (a b c)"))), "s_stp", reads=[B["stp"]])
                        S.dma("sp", (lambda e, l=l: e.dma_start(out=oc_out[l], in_=stc_t[:].rearrange("p a b c -> p (a b c)"))), "s_stc", reads=[B["stc"]])
                        S.dma("sp", (lambda e, l=l: e.dma_start(out=os_out[l], in_=sts_t[:].rearrange("p a b c -> p (a b c)"))), "s_sts", reads=[B["sts"]])
                        S.dma("sp", (lambda e, l=l: e.dma_start(out=of_out[l], in_=stf_t[:].rearrange("p a b c d -> p (a b c d)"))), "s_stf", reads=[B["stf"]])

                rmsnorm("nfin", 0, "cat", presq=True, final=True)
                S.dma("sp", (lambda e: e.dma_start(out=yT[:, 0:8, c0:c0 + WP], in_=yv0[:, :, 0:WP])), "s_y0", reads=xnB)
                if Ws:
                    S.dma("sp", (lambda e: e.dma_start(out=yT[:, 0:8, TPC:TC], in_=yv0[:, :, WP:W])), "s_y0", reads=xnB)
                S.dma("sp", (lambda e: e.dma_start(out=yT[:, 8:16, c0:c0 + WP], in_=yv1[:, :, 0:WP])), "s_y1", reads=[B["cat"]])
                if Ws:
                    S.dma("sp", (lambda e: e.dma_start(out=yT[:, 8:16, TPC:TC], in_=yv1[:, :, WP:W])), "s_y1", reads=[B["cat"]])

            fin = [(s, S.cnt[s]) for s in ("s_y0", "s_y1", "s_ost0", "s_ost1", "s_stp", "s_stc", "s_sts", "s_stf")]
            S.wait("sp", fin)
            assert ring.cur == len(ring.jobs)

        S0 = Sched()
        program(S0)
        ring.dry = False
        S = Sched()
        program(S)

        @block.tensor
        def _(e):
            S.emit("pe", e, sems)

        @block.scalar
        def _(e):
            S.emit("act", e, sems)

        @block.vector
        def _(e):
            S.emit("dve", e, sems)

        @block.gpsimd
        def _(e):
            S.emit("pool", e, sems)

        @block.sync
        def _(e):
            S.emit("sp", e, sems)
    return nc


def _fm(X):
    ncol, nf = X.shape
    return np.ascontiguousarray(X.T.reshape(nf // 128, 128, ncol).transpose(1, 0, 2))


def _unfm(A):
    p, k, n = A.shape
    return np.ascontiguousarray(A.transpose(2, 1, 0).reshape(n, k * 128))


def _vec(v):
    return np.ascontiguousarray(v.reshape(-1, 128).T)


def prepare_inputs(inp):
    f = lambda a: np.asarray(a, dtype=np.float32)
    x_prompt, x_sample = f(inp["x_prompt"]), f(inp["x_sample"])
    p_prompt, p_sample = f(inp["p_prompt"]), f(inp["p_sample"])
    st_pool, st_conf, st_sc, st_ffn = f(inp["state_pool"]), f(inp["state_conf"]), f(inp["state_sc"]), f(inp["state_ffn"])
    def tiles(w):
        l_, k_, m_ = w.shape
        return np.ascontiguousarray(w.reshape(l_, k_ // 128, 128, m_ // 128, 128).transpose(0, 3, 2, 1, 4)).reshape(l_ * (m_ // 128), 128, k_)
    shared = {
        "w_in": tiles(f(inp["w_in"])),
        "w_pool": f(inp["w_pool"]).reshape(L * 512, 128),
        "conf_pw": tiles(f(inp["conf_pw"])),
        "w_out": tiles(f(inp["w_out"])),
        "w_up": tiles(f(inp["w_up"])),
        "w_down": tiles(f(inp["w_down"])),
        "ple_gate": tiles(f(inp["ple_gate"])),
        "ple_proj": f(inp["ple_proj"]).reshape(L * 256, D),
    }
    cbase = np.zeros((128, NCST), np.float32)

    def put(name, arr):
        arr = arr.reshape(128, -1)
        cbase[:, COFF[name]:COFF[name] + arr.shape[1]] = arr
    put("ident", np.eye(128, dtype=np.float32))
    put("nmix", np.stack([_vec(f(inp["norm_mix"])[l]) for l in range(L)], 1))
    put("nffn", np.stack([_vec(f(inp["norm_ffn"])[l]) for l in range(L)], 1))
    put("nple", np.stack([_vec(f(inp["norm_ple"])[l]) for l in range(L)], 1))
    put("nfin", _vec(f(inp["norm_final"])))
    put("pscale", np.stack([_vec(f(inp["pool_scale"])[l]) for l in range(L)], 1))
    put("cdw", np.stack([f(inp["conf_dw"])[l].reshape(31, 6, 128).transpose(2, 1, 0) for l in range(L)], 1))
    put("cdb", np.stack([_vec(f(inp["conf_dw_b"])[l]) for l in range(L)], 1))
    put("lng", np.stack([_vec(f(inp["conf_ln_g"])[l]) for l in range(L)], 1))
    put("lnb", np.stack([_vec(f(inp["conf_ln_b"])[l]) for l in range(L)], 1))
    put("pwb", np.stack([_vec(f(inp["conf_pw_b"])[l]) for l in range(L)], 1))
    put("scw", np.stack([f(inp["sc_conv"])[l].reshape(3, 6, 128).transpose(2, 1, 0) for l in range(L)], 1))
    put("ffw", np.stack([f(inp["ffn_conv"])[l].reshape(3, 2, FC, 128).transpose(3, 2, 1, 0) for l in range(L)], 1))

    def hist_layout(stl, nch, kh):
        a = stl.transpose(2, 1, 0).reshape(nch, 128, kh, 16).transpose(1, 0, 2, 3)
        out = np.zeros((128, nch, kh + 1, 16), np.float32)
        out[:, :, 0:kh - 1, :] = a[:, :, 1:kh, :]
        out[:, :, kh, :] = a[:, :, 0, :]
        return out

    in_maps = []
    for c in range(NCORE):
        b, half = c // 2, c % 2
        t0 = 0 if half == 0 else 2048 - TPC
        sl = slice(NS * c, NS * (c + 1))
        m = dict(shared)
        m["xT"] = _fm(np.concatenate([x_prompt[b, t0:t0 + TPC], x_sample[sl, 0]], 0))
        m["pT"] = np.stack([_fm(np.concatenate([p_prompt[l, b, t0:t0 + TPC], p_sample[l, sl, 0]], 0)) for l in range(L)], 0)
        m["hp"] = np.stack([hist_layout(st_pool[l, sl], 4, 15) for l in range(L)], 0).reshape(L, 128, -1)
        m["hc"] = np.stack([hist_layout(st_conf[l, sl], 6, 30) for l in range(L)], 0).reshape(L, 128, -1)
        m["hs"] = np.stack([hist_layout(st_sc[l, sl], 6, 2) for l in range(L)], 0).reshape(L, 128, -1)
        hf = []
        for l in range(L):
            a = hist_layout(st_ffn[l, sl], 2 * FC, 2)
            hf.append(a.reshape(128, 2, FC, 3, 16).transpose(0, 2, 1, 3, 4))
        m["hf"] = np.ascontiguousarray(np.stack(hf, 0)).reshape(L, 128, -1)
        cc = cbase.copy()
        invc = np.zeros((4, 16), np.float32)
        for g, w in enumerate(POOLW):
            for i in range(16):
                pos = t0 + i
                invc[g, i] = 1.0 / min(w, pos + 1)
        cc[:, COFF["invc"]:COFF["invc"] + 64] = invc.reshape(1, 64)
        m["cst"] = cc
        in_maps.append(m)
    return in_maps


def assemble(results):
    y_prompt = np.zeros((4, 2048, D), np.float32)
    y_sample = np.zeros((128, 1, D), np.float32)
    npp = np.zeros((L, 4, 15, 512), np.float32)
    nps = np.zeros((L, 128, 15, 512), np.float32)
    ncp = np.zeros((L, 4, 30, 768), np.float32)
    ncs = np.zeros((L, 128, 30, 768), np.float32)
    nsp = np.zeros((L, 4, 2, 768), np.float32)
    nss = np.zeros((L, 128, 2, 768), np.float32)
    nfp = np.zeros((L, 4, 2, 2 * DFF), np.float32)
    nfs = np.zeros((L, 128, 2, 2 * DFF), np.float32)
    for c in range(NCORE):
        r = results[c]
        b, half = c // 2, c % 2
        sl = slice(NS * c, NS * (c + 1))
        Y = _unfm(np.asarray(r["yT"]))
        if half == 0:
            y_prompt[b, 0:TPC] = Y[0:TPC]
        else:
            y_prompt[b, TPC:2048] = Y[2 * TPC - 2048:TPC]
        y_sample[sl, 0] = Y[TPC:TC]
        pst = np.asarray(r["pst"])
        ops_ = np.asarray(r["ops"]).reshape(L, 128, 4, 16, 16)
        ocs_ = np.asarray(r["ocs"]).reshape(L, 128, 6, 31, 16)
        oss_ = np.asarray(r["oss"]).reshape(L, 128, 6, 3, 16)
        ofs_ = np.asarray(r["ofs"]).reshape(L, 128, FC, 2, 3, 16)
        for l in range(L):
            if half == 1:
                npp[l, b] = pst[l][:, OFF_P:OFF_P + 60].reshape(128, 4, 15).transpose(2, 1, 0).reshape(15, 512)
                ncp[l, b] = pst[l][:, OFF_C:OFF_C + 180].reshape(128, 6, 30).transpose(2, 1, 0).reshape(30, 768)
                nsp[l, b] = pst[l][:, OFF_S:OFF_S + 12].reshape(128, 6, 2).transpose(2, 1, 0).reshape(2, 768)
                nfp[l, b] = pst[l][:, OFF_F:OFF_F + 176].reshape(128, FC, 2, 2).transpose(3, 2, 1, 0).reshape(2, 2 * DFF)
            nps[l, sl] = ops_[l][:, :, 0:15, :].transpose(3, 2, 1, 0).reshape(16, 15, 512)
            ncs[l, sl] = ocs_[l][:, :, 0:30, :].transpose(3, 2, 1, 0).reshape(16, 30, 768)
            nss[l, sl] = oss_[l][:, :, 0:2, :].transpose(3, 2, 1, 0).reshape(16, 2, 768)
            nfs[l, sl] = ofs_[l][:, :, :, 0:2, :].transpose(4, 3, 2, 1, 0).reshape(16, 2, 2 * DFF)
    return (y_prompt, y_sample, npp, nps, ncp, ncs, nsp, nss, nfp, nfs)


_NC = None


def kernel(**inputs):
    global _NC
    if _NC is None:
        _NC = build_nc()
    in_maps = prepare_inputs(inputs)
    res = run_bass_kernel_spmd(_NC, in_maps, core_ids=list(range(NCORE)))
    return assemble(res.results)
```

```python
import numpy as np
from contextlib import ExitStack
import concourse.bass as bass
import concourse.mybir as mybir
from concourse.bass_utils import run_bass_kernel_spmd

F32 = mybir.dt.float32
BF16 = mybir.dt.bfloat16
AF = mybir.ActivationFunctionType
ALU = mybir.AluOpType
AX = mybir.AxisListType

L = 2
D = 2048
KC = 16
DFF = 5632
FC = 44
WP = 528
NS = 16
TPC = 1056
TC = TPC + NS
NCORE = 8
EPS = 1e-6
POOLW = (2, 4, 8, 16)
NWB = 5
GS = 11
NDT = 9
ENGS = ("pe", "act", "dve", "pool", "sp")

OFF_P, OFF_C, OFF_S, OFF_F, NST = 0, 60, 240, 252, 428


class Buf:
    def __init__(self, name, sem=None):
        self.name = name
        self.sem = sem
        self.w = None
        self.r = {}


class _Rec:
    def __init__(self):
        self.calls = []

    def __getattr__(self, name):
        def f(*a, **kw):
            self.calls.append((name, a, kw))
            return self
        return f


def _record(fn):
    r = _Rec()
    fn(r)
    return r.calls


class Sched:
    def __init__(self):
        self.q = {e: [] for e in ENGS}
        self.cnt = {e: 0 for e in ENGS}
        self.waited = {e: {} for e in ENGS}

    def _deps(self, reads, writes, extra):
        deps = list(extra)
        for b in reads:
            deps.append(b.w)
        for b in writes:
            deps.append(b.w)
            deps.extend(b.r.items())
        return deps

    def _waits(self, eng, deps):
        waits = []
        for d in deps:
            if d is None:
                continue
            s, v = d
            if v <= 0 or self.waited[eng].get(s, 0) >= v or (eng == "pe" and s == "pe"):
                continue
            self.waited[eng][s] = v
            waits.append((s, v))
        return waits

    def _mark(self, tok, reads, writes):
        for b in reads:
            b.r[tok[0]] = max(b.r.get(tok[0], 0), tok[1])
        for b in writes:
            b.w = tok
            b.r = {}

    def op(self, eng, fn, reads=(), writes=(), extra=()):
        waits = self._waits(eng, self._deps(reads, writes, extra))
        self.cnt[eng] += 1
        self.q[eng].append((_record(fn), waits, (eng, 1)))
        tok = (eng, self.cnt[eng])
        self._mark(tok, reads, writes)
        return tok

    def dma(self, eng, fn, sem, reads=(), writes=(), extra=()):
        waits = self._waits(eng, self._deps(reads, writes, extra))
        self.cnt[sem] = self.cnt.get(sem, 0) + 16
        self.q[eng].append((_record(fn), waits, (sem, 16)))
        tok = (sem, self.cnt[sem])
        self._mark(tok, reads, writes)
        return tok

    def wait(self, eng, deps):
        waits = self._waits(eng, deps)
        if waits:
            self.q[eng].append((None, waits, None))

    def barrier(self, engs=("pe", "act", "dve")):
        toks = [(o, self.cnt[o]) for o in ("pe", "act", "dve")]
        for e in engs:
            self.wait(e, toks)

    def emit(self, eng, e, sems):
        for fn, waits, inc in self.q[eng]:
            for s, v in waits:
                e.wait_ge(sems[s], v)
            if fn is not None:
                ins = None
                for name, a, kw in fn:
                    ins = getattr(e, name)(*a, **kw)
                if inc is not None:
                    ins.then_inc(sems[inc[0]], inc[1])


def _pack_consts():
    segs = [("nmix", L * 16), ("nffn", L * 16), ("nple", L * 16), ("nfin", 16), ("pscale", L * 4),
            ("cdw", L * 6 * 31), ("cdb", L * 6), ("lng", L * 6), ("lnb", L * 6), ("pwb", L * 6),
            ("scw", L * 6 * 3), ("ffw", L * FC * 2 * 3), ("invc", 64), ("ident", 128)]
    off, o = {}, 0
    for n, sz in segs:
        off[n] = o
        o += sz
    return off, o


COFF, NCST = _pack_consts()


def build_nc():
    nc = bass.Bass("TRN2", target_bir_lowering=False)
    dt_in = lambda n, s: nc.dram_tensor(n, s, F32, kind="ExternalInput").ap()
    dt_out = lambda n, s: nc.dram_tensor(n, s, F32, kind="ExternalOutput").ap()
    xT = dt_in("xT", [128, KC, TC])
    pT = dt_in("pT", [L, 128, 2, TC])
    hp_in = dt_in("hp", [L, 128, 4 * 16 * 16])
    hc_in = dt_in("hc", [L, 128, 6 * 31 * 16])
    hs_in = dt_in("hs", [L, 128, 6 * 3 * 16])
    hf_in = dt_in("hf", [L, 128, FC * 2 * 3 * 16])
    cst_in = dt_in("cst", [128, NCST])
    w_in = dt_in("w_in", [L * 34, 128, D])
    w_pool = dt_in("w_pool", [L * 4 * 128, 128])
    conf_pw = dt_in("conf_pw", [L * 6, 128, 768])
    w_out = dt_in("w_out", [L * KC, 128, D])
    w_up = dt_in("w_up", [L * 2 * FC, 128, D])
    w_down = dt_in("w_down", [L * KC, 128, DFF])
    ple_gate = dt_in("ple_gate", [L * KC, 128, D])
    ple_proj = dt_in("ple_proj", [L * 256, D])
    yT = dt_out("yT", [128, KC, TC])
    pst_out = dt_out("pst", [L, 128, NST])
    op_out = dt_out("ops", [L, 128, 4 * 16 * 16])
    oc_out = dt_out("ocs", [L, 128, 6 * 31 * 16])
    os_out = dt_out("oss", [L, 128, 6 * 3 * 16])
    of_out = dt_out("ofs", [L, 128, FC * 2 * 3 * 16])

    WMAX = WP + NS
    with ExitStack() as es:
        sb = lambda n, shp, dt=F32: es.enter_context(nc.sbuf_tensor(n, shp, dt))
        h_t = sb("h", [128, KC, WMAX])
        xn_t = sb("xn", [128, KC, WMAX], BF16)
        cat_t = sb("cat", [128, KC, WMAX], BF16)
        wb_t = sb("wb", [128, NWB, 2048], BF16)
        ptb_t = sb("ptb", [128, 2, WMAX], BF16)
        wpl_t = sb("wpl", [128, 4, 128], BF16)
        cst_t = sb("cstt", [128, NCST])
        carry_t = sb("carry", [128, L, NST])
        ost_t = sb("ost", [128, L, NST])
        zero_t = sb("zero", [128, 32])
        ones_t = sb("ones", [128, 128], BF16)
        ones_f = sb("onesf", [128, 2])
        warm_t = sb("warm", [128, 2])
        stp_t = sb("stp", [128, 4, 16, 16])
        stc_t = sb("stc", [128, 6, 31, 16])
        sts_t = sb("sts", [128, 6, 3, 16])
        stf_t = sb("stf", [128, FC, 2, 3, 16])
        rstd_t = sb("rstd", [128, WMAX])
        NU = 14848 + 1024 + 224 + 448
        U_t = sb("U", [128, NU])
        ps_t = es.enter_context(nc.psum_tensor("ps", [128, 8, 512], F32))

        dma_sems = ["s_y0", "s_y1", "s_hx0", "s_hx1", "s_hx2", "s_hx3", "s_h", "s_ptb", "s_wpl", "s_cst", "s_ptw", "s_ost0", "s_ost1", "s_stp", "s_stc", "s_sts", "s_stf"] + [f"s_w{i}" for i in range(NWB)]
        sems = {n: es.enter_context(nc.semaphore(n)) for n in list(ENGS) + dma_sems}
        block = es.enter_context(nc.Block())

        class Arena:
            def __init__(self):
                self.o = 0

            def f32(self, n):
                a = U_t[:, self.o:self.o + n]
                self.o += n
                assert self.o <= NU, ("arena overflow", self.o)
                return a

            def bf16(self, n):
                m = (n + 1) // 2
                return self.f32(m).bitcast(BF16)[:, 0:n]

        class WRing:
            def __init__(self):
                self.jobs = []
                self.dry = True

            def reset(self, S):
                self.S = S
                self.cur = 0
                self.rec = 0
                self.hold = 0
                self.bufs = [Buf(f"w{i}", f"s_w{i}") for i in range(NWB)]

            def next(self, mat, t, k0, nk):
                if self.dry:
                    self.jobs.append((mat, t, k0, nk))
                i = self.cur
                self.cur += 1
                assert self.jobs[i][1:] == (t, k0, nk)
                while self.rec < min(i + NWB - self.hold, len(self.jobs)):
                    n = self.rec
                    m_, t_, k0_, k_ = self.jobs[n]
                    slot = n % NWB
                    src = m_[t_, :, k0_ * 128:(k0_ + k_) * 128]
                    dst = wb_t[:, slot, 0:k_ * 128]
                    self.S.dma("pool", (lambda e, dst=dst, src=src: e.dma_start(out=dst, in_=src)),
                               f"s_w{slot}", writes=[self.bufs[slot]])
                    self.rec += 1
                slot = i % NWB
                view = wb_t[:, slot, 0:nk * 128].rearrange("p (k m) -> p k m", k=nk)
                return self.bufs[slot], view

        ring = WRing()

        def program(S):
            ring.reset(S)
            B = {n: Buf(n, s) for n, s in [("h", "s_h"), ("xn", None), ("cat", None), ("ptb", "s_ptb"), ("wpl", "s_wpl"),
                                           ("cst", "s_cst"), ("ptw", "s_ptw"), ("carry", None), ("ost0", "s_ost0"), ("ost1", "s_ost1"),
                                           ("zero", None), ("ones", None), ("onesf", None), ("warm", None), ("sqtop", None), ("stp", "s_stp"), ("stc", "s_stc"), ("sts", "s_sts"),
                                           ("stf", "s_stf"), ("rstd", None)]}
            psb = [Buf(f"ps{i}") for i in range(4)]
            xnB = [Buf(f"xn{i}") for i in range(8)]
            catB = [Buf(f"catq{i}") for i in range(8)]
            hB = [Buf(f"h{i}", "s_h") for i in range(KC)]
            sqB = [Buf(f"sq{i}") for i in range(4)]
            st = {"psj": 0, "fresh_xn": False}

            sqtop_v = U_t[:, NU - KC * WMAX // 2:NU].bitcast(BF16).rearrange("p (k w) -> p k w", k=KC)
            yv0 = xn_t[:].rearrange("p k w -> p (k w)").bitcast(F32).rearrange("p (k w) -> p k w", k=8)
            yv1 = cat_t[:].rearrange("p k w -> p (k w)").bitcast(F32).rearrange("p (k w) -> p k w", k=8)

            def cs(name, i0, n=1):
                o = COFF[name] + i0
                return cst_t[:, o:o + n]

            S.dma("sp", lambda e: e.dma_start(out=cst_t[:], in_=cst_in), "s_cst", writes=[B["cst"]])
            S.op("dve", lambda e: e.memset(zero_t[:], 0.0), writes=[B["zero"]])
            S.op("dve", lambda e: e.memset(ones_t[:], 1.0), writes=[B["ones"]])
            S.op("dve", lambda e: e.memset(ones_f[:], 1.0), writes=[B["onesf"]])

            for pi in range(2):
                if pi == 0:
                    S.op("act", (lambda e: e.activation(out=warm_t[:, 0:1], in_=ones_f[:, 0:1], func=AF.Ln)), reads=[B["onesf"]], writes=[B["warm"]])
                Ws = NS if pi == 1 else 0
                W = WP + Ws
                Wh = W // 2
                c0 = pi * WP

                def v2(ap):
                    return ap.rearrange("p (a b) -> p a b", a=2)

                def psv(slot):
                    return ps_t[:, 2 * slot:2 * slot + 2, 0:Wh]

                def mm_job(nk, lhs, rhs, reads, kreads=None):
                    slot = st["psj"] % 4
                    st["psj"] += 1
                    groups = kreads if kreads else [(0, nk, [])]
                    for (k0, k1, extra_reads) in groups:
                        def fn(e, slot=slot, nk=nk, lhs=lhs, rhs=rhs, k0=k0, k1=k1):
                            for k in range(k0, k1):
                                for t in range(2):
                                    e.matmul(ps_t[:, 2 * slot + t, 0:Wh], lhsT=lhs(k), rhs=rhs(k)[:, t * Wh:(t + 1) * Wh],
                                             start=(k == 0), stop=(k == nk - 1))
                        S.op("pe", fn, reads=list(reads) + list(extra_reads), writes=[psb[slot]])
                    return slot

                def linear(mat, t, nk, rhs_t, rhs_buf):
                    wbuf, wv = ring.next(mat, t, 0, nk)
                    if rhs_buf == "xn":
                        if st["fresh_xn"]:
                            st["fresh_xn"] = False
                            return mm_job(nk, (lambda k, wv=wv: wv[:, k, :]), (lambda k, rhs_t=rhs_t: rhs_t[:, k, 0:W]), [wbuf],
                                          kreads=[(2 * i, 2 * i + 2, [xnB[i]]) for i in range(8)])
                        return mm_job(nk, (lambda k, wv=wv: wv[:, k, :]), (lambda k, rhs_t=rhs_t: rhs_t[:, k, 0:W]), [wbuf] + xnB)
                    return mm_job(nk, (lambda k, wv=wv: wv[:, k, :]), (lambda k, rhs_t=rhs_t: rhs_t[:, k, 0:W]), [wbuf, rhs_buf])

                def linear_multi(specs):
                    st["fresh_xn"] = False
                    ring.hold = len(specs) - 1
                    jobs = []
                    for (mat, t) in specs:
                        wbuf, wv = ring.next(mat, t, 0, KC)
                        slot = st["psj"] % 4
                        st["psj"] += 1
                        jobs.append((wbuf, wv, slot))
                    for g in range(8):
                        for (wbuf, wv, slot) in jobs:
                            def fn(e, wv=wv, slot=slot, g=g):
                                for k in (2 * g, 2 * g + 1):
                                    for t in range(2):
                                        e.matmul(ps_t[:, 2 * slot + t, 0:Wh], lhsT=wv[:, k, :], rhs=xn_t[:, k, t * Wh:(t + 1) * Wh],
                                                 start=(k == 0), stop=(k == KC - 1))
                            S.op("pe", fn, reads=[wbuf, xnB[g]], writes=[psb[slot]])
                    ring.hold = 0
                    return [j[2] for j in jobs]

                def sq_region(region):
                    if region == "xn":
                        return xn_t, (lambda k: [xnB[k // 2]])
                    if region == "top":
                        return sqtop_v, (lambda k: [B["sqtop"]])
                    return cat_t, (lambda k: [catB[k // 2]])

                def xnp_chunk(m, gname, l):
                    g = cs(gname, l * 16 + m)
                    S.op("act", (lambda e: e.activation(out=xn_t[:, m, 0:W], in_=h_t[:, m, 0:W], func=AF.Identity, scale=g)),
                         reads=[hB[m], B["cst"]], writes=[xnB[m // 2]])

                def rmsnorm_stats(region):
                    sqv, bf = sq_region(region)
                    early = []
                    for k in range(0, 14, 2):
                        early += bf(k)
                    slot = mm_job(KC, (lambda k: ones_t[:]), (lambda k: sqv[:, k, 0:W]), [B["ones"]], kreads=[(0, 14, early), (14, 16, bf(14))])
                    S.op("act", (lambda e: e.activation(out=v2(rstd_t[:, 0:W]), in_=psv(slot), func=AF.Ln, scale=1.0 / D, bias=EPS)),
                         reads=[psb[slot]], writes=[B["rstd"]])
                    S.op("act", (lambda e: e.activation(out=rstd_t[:, 0:W], in_=rstd_t[:, 0:W], func=AF.Exp, scale=-0.5)), reads=[B["rstd"]], writes=[B["rstd"]])

                def warm_lnexp():
                    S.op("act", (lambda e: e.activation(out=warm_t[:, 0:1], in_=ones_f[:, 0:1], func=AF.Ln)), reads=[B["onesf"]], writes=[B["warm"]])

                def sq_chunk(m, region):
                    sqv, bf = sq_region(region)
                    ex = ([B["cat"].w] + list(B["cat"].r.items())) if region == "cat" else []
                    S.op("act", (lambda e: e.activation(out=sqv[:, m, 0:W], in_=h_t[:, m, 0:W], func=AF.Square)), reads=[hB[m]], writes=bf(m), extra=ex)

                def rmsnorm(gname, l, region, presq, final=False):
                    sqv, bf = sq_region(region)
                    if not presq:
                        for q in range(4):
                            S.op("act", (lambda e, q=q: e.activation(out=sqv[:, 4 * q:4 * q + 4, 0:W], in_=h_t[:, 4 * q:4 * q + 4, 0:W], func=AF.Square)),
                                 reads=hB[4 * q:4 * q + 4], writes=bf(4 * q) + bf(4 * q + 2))
                    if presq:
                        early = []
                        for k in range(0, 14, 2):
                            early += bf(k)
                        kr = [(0, 14, early), (14, 16, bf(14))]
                    else:
                        kr = [(4 * i, 4 * i + 4, bf(4 * i) + bf(4 * i + 2)) for i in range(4)]
                    slot = mm_job(KC, (lambda k: ones_t[:]), (lambda k: sqv[:, k, 0:W]), [B["ones"]], kreads=kr)
                    S.op("act", (lambda e, slot=slot: e.activation(out=v2(rstd_t[:, 0:W]), in_=psv(slot), func=AF.Ln, scale=1.0 / D, bias=EPS)),
                         reads=[psb[slot]], writes=[B["rstd"]])
                    S.op("act", (lambda e: e.activation(out=rstd_t[:, 0:W], in_=rstd_t[:, 0:W], func=AF.Exp, scale=-0.5)), reads=[B["rstd"]], writes=[B["rstd"]])
                    for k in range(KC):
                        g = cs(gname, (0 if final else l * 16) + k)
                        if final:
                            yv = yv0[:, k, 0:W] if k < 8 else yv1[:, k - 8, 0:W]
                            S.op("dve", (lambda e, k=k, g=g, yv=yv: e.scalar_tensor_tensor(out=yv, in0=h_t[:, k, 0:W], scalar=g, in1=rstd_t[:, 0:W],
                                                                                            op0=ALU.mult, op1=ALU.mult)),
                                 reads=[hB[k], B["rstd"], B["cst"]], writes=([xnB[k]] if k < 8 else [B["cat"]]))
                        else:
                            S.op("dve", (lambda e, k=k, g=g: e.scalar_tensor_tensor(out=xn_t[:, k, 0:W], in0=h_t[:, k, 0:W], scalar=g, in1=rstd_t[:, 0:W],
                                                                                     op0=ALU.mult, op1=ALU.mult)),
                                 reads=[hB[k], B["rstd"], B["cst"]], writes=[xnB[k // 2]])
                    if not final:
                        st["fresh_xn"] = True

                if Ws:
                    S.dma("sp", (lambda e: e.dma_start(out=h_t[:, :, WP:W], in_=xT[:, :, TPC:TC])), "s_h", writes=hB)
                for q in range(4):
                    S.dma("sp", (lambda e, q=q: e.dma_start(out=h_t[:, 4 * q:4 * q + 4, 0:WP], in_=xT[:, 4 * q:4 * q + 4, c0:c0 + WP])), f"s_hx{q}", writes=hB[4 * q:4 * q + 4])

                if pi == 0:
                    S.wait("pool", [("s_hx2", 16)])

                for l in range(L):
                    def hist(off, n, l=l):
                        if pi == 0:
                            return zero_t[:, 0:n]
                        return carry_t[:, l, off:off + n]
                    hbuf = B["zero"] if pi == 0 else B["carry"]

                    def savedst(off, n, l=l):
                        return (carry_t if pi == 0 else ost_t)[:, l, off:off + n]
                    sbuf_ = B["carry"] if pi == 0 else B[f"ost{l}"]

                    S.dma("pool", (lambda e, l=l, c0=c0: e.dma_start(out=ptb_t[:, :, 0:WP], in_=pT[l, :, :, c0:c0 + WP])), "s_ptb", writes=[B["ptb"]])
                    if Ws:
                        S.dma("pool", (lambda e, l=l: e.dma_start(out=ptb_t[:, :, WP:W], in_=pT[l, :, :, TPC:TC])), "s_ptb", writes=[B["ptb"]])
                    S.dma("pool", (lambda e, l=l: e.dma_start(out=wpl_t[:], in_=w_pool[l * 512:(l + 1) * 512, :].rearrange("(g c) d -> c g d", c=128))),
                          "s_wpl", writes=[B["wpl"]])
                    if Ws:
                        S.dma("sp", (lambda e, l=l: e.dma_start(out=stp_t[:].rearrange("p a b c -> p (a b c)"), in_=hp_in[l])), "s_stp", writes=[B["stp"]])
                        S.dma("sp", (lambda e, l=l: e.dma_start(out=stc_t[:].rearrange("p a b c -> p (a b c)"), in_=hc_in[l])), "s_stc", writes=[B["stc"]])
                        S.dma("sp", (lambda e, l=l: e.dma_start(out=sts_t[:].rearrange("p a b c -> p (a b c)"), in_=hs_in[l])), "s_sts", writes=[B["sts"]])
                        S.dma("sp", (lambda e, l=l: e.dma_start(out=stf_t[:].rearrange("p a b c d -> p (a b c d)"), in_=hf_in[l])), "s_stf", writes=[B["stf"]])

                    S.barrier(("act", "dve"))
                    rmsnorm("nmix", l, ("cat" if l > 0 else "xn"), presq=(l > 0))
                    ar = Arena()
                    sig_a = [ar.f32(WMAX) for _ in range(2)]
                    G_a = [ar.f32(30 + WMAX) for _ in range(2)]
                    Gb_a = [ar.bf16(30 + WMAX) for _ in range(2)]
                    D_a = ar.bf16(31 * 128).rearrange("p (k m) -> p k m", k=31)
                    CB_a = ar.f32(6 * WMAX).rearrange("p (j w) -> p j w", j=6)
                    sil_a = ar.bf16(6 * WMAX).rearrange("p (j w) -> p j w", j=6)
                    tmpc_a = ar.f32(16 * 29).rearrange("p (b s) -> p b s", b=16)
                    r1_a = ar.f32(16)
                    hb_a = [ar.f32(WMAX) for _ in range(2)]
                    V_a = [ar.f32(2 + WMAX) for _ in range(2)]
                    cv_a = [ar.f32(WMAX) for _ in range(2)]
                    U_a = [ar.f32(15 + WMAX) for _ in range(2)]
                    S_a = [ar.f32(15 + WMAX) for _ in range(2)]
                    dbf_a = [ar.bf16(WMAX) for _ in range(2)]
                    t16_a = ar.f32(16)
                    sigB = [Buf("sig0"), Buf("sig1")]
                    GB = [Buf("G0"), Buf("G1")]
                    GbB = [Buf("Gb0"), Buf("Gb1")]
                    DB = Buf("D")
                    CBBj = [Buf(f"CB{j}") for j in range(6)]
                    silB = Buf("sil")
                    tmpB = Buf("tmpc")
                    hbB = [Buf("hb0"), Buf("hb1")]
                    VB = [Buf("V0"), Buf("V1")]
                    cvB = [Buf("cv0"), Buf("cv1")]
                    UB = [Buf("U0"), Buf("U1")]
                    SB_ = [Buf("S0"), Buf("S1")]
                    dB = [Buf("d0"), Buf("d1")]
                    ident_v = cs("ident", 0, 128)

                    prez = dict(zip((4, 10, 5), linear_multi([(w_in, l * 34 + 4), (w_in, l * 34 + 10), (w_in, l * 34 + 5)])))

                    def zin(m, l=l):
                        if m in prez:
                            return prez.pop(m)
                        return linear(w_in, l * 34 + m, KC, xn_t, "xn")

                    def conf_pair(j):
                        i2 = j % 2
                        sa_ = zin(4 + j)
                        sb_ = zin(10 + j)
                        G, sg, Gb = G_a[i2], sig_a[i2], Gb_a[i2]
                        S.op("act", (lambda e: e.activation(out=v2(sg[:, 0:W]), in_=psv(sb_), func=AF.Sigmoid)),
                             reads=[psb[sb_]], writes=[sigB[i2]])
                        S.op("dve", (lambda e: e.tensor_copy(out=G[:, 0:30], in_=hist(OFF_C + 30 * j, 30))), reads=[hbuf], writes=[GB[i2]])
                        S.op("dve", (lambda e: e.tensor_tensor(out=v2(G[:, 30:30 + W]), in0=psv(sa_), in1=v2(sg[:, 0:W]), op=ALU.mult)),
                             reads=[psb[sa_], sigB[i2]], writes=[GB[i2]])
                        S.op("act", (lambda e: e.activation(out=Gb[:, 0:30 + W], in_=G[:, 0:30 + W], func=AF.Copy)), reads=[GB[i2]], writes=[GbB[i2]])
                        S.op("dve", (lambda e: e.tensor_copy(out=savedst(OFF_C + 30 * j, 30), in_=G[:, WP:WP + 30])), reads=[GB[i2]], writes=[sbuf_])
                        if Ws:
                            S.op("dve", (lambda e: e.tensor_copy(out=stc_t[:, j, 29, :], in_=G[:, 30 + WP:30 + W])), reads=[GB[i2]], writes=[B["stc"]])

                    def conf_dgen(j):
                        wo = COFF["cdw"] + (l * 6 + j) * 31
                        wbc = cst_t[:, wo + NDT:wo + 31].unsqueeze(2).broadcast_to([128, 31 - NDT, 128])
                        idb = ident_v.unsqueeze(1).broadcast_to([128, 31 - NDT, 128])
                        S.op("dve", (lambda e: e.tensor_tensor(out=D_a[:, NDT:31, :], in0=idb, in1=wbc, op=ALU.mult)), reads=[B["cst"]], writes=[DB])

                    def conf_conv(j):
                        i2 = j % 2
                        Gb = Gb_a[i2]
                        G = G_a[i2]
                        wo = COFF["cdw"] + (l * 6 + j) * 31
                        S.op("dve", (lambda e: e.tensor_scalar(out=CB_a[:, j, 0:WP], in0=G[:, 0:WP], scalar1=cst_t[:, wo:wo + 1], scalar2=None, op0=ALU.mult)),
                             reads=[GB[i2], B["cst"]], writes=[CBBj[j]])
                        for k in range(1, NDT):
                            S.op("dve", (lambda e, k=k: e.scalar_tensor_tensor(out=CB_a[:, j, 0:WP], in0=G[:, k:k + WP], scalar=cst_t[:, wo + k:wo + k + 1],
                                                                                in1=CB_a[:, j, 0:WP], op0=ALU.mult, op1=ALU.add)),
                                 reads=[GB[i2], B["cst"]], writes=[CBBj[j]])
                        slot = st["psj"] % 4
                        st["psj"] += 1

                        def fn(e):
                            order = [30] + list(range(NDT, 30))
                            for n_, k in enumerate(order):
                                for t in range(2):
                                    lo = t * Wh
                                    hi = (t + 1) * Wh if k == 30 else min((t + 1) * Wh, WP)
                                    e.matmul(ps_t[:, 2 * slot + t, 0:hi - lo], lhsT=D_a[:, k, :], rhs=Gb[:, k + lo:k + hi],
                                             start=(n_ == 0), stop=(n_ == len(order) - 1))
                        S.op("pe", fn, reads=[DB, GbB[i2]], writes=[psb[slot]])
                        bias_ = cs("cdb", l * 6 + j)
                        if Ws:
                            S.op("dve", (lambda e: e.scalar_tensor_tensor(out=CB_a[:, j, 0:Wh], in0=ps_t[:, 2 * slot, 0:Wh], scalar=bias_, in1=CB_a[:, j, 0:Wh],
                                                                          op0=ALU.add, op1=ALU.add)), reads=[psb[slot], B["cst"]], writes=[CBBj[j]])
                            S.op("dve", (lambda e: e.scalar_tensor_tensor(out=CB_a[:, j, Wh:WP], in0=ps_t[:, 2 * slot + 1, 0:WP - Wh], scalar=bias_, in1=CB_a[:, j, Wh:WP],
                                                                          op0=ALU.add, op1=ALU.add)), reads=[psb[slot], B["cst"]], writes=[CBBj[j]])
                            S.op("act", (lambda e: e.activation(out=CB_a[:, j, WP:W], in_=ps_t[:, 2 * slot + 1, WP - Wh:Wh], func=AF.Identity, bias=bias_)),
                                 reads=[psb[slot], B["cst"]], writes=[CBBj[j]])
                        else:
                            S.op("dve", (lambda e: e.scalar_tensor_tensor(out=v2(CB_a[:, j, 0:W]), in0=psv(slot), scalar=bias_, in1=v2(CB_a[:, j, 0:W]),
                                                                          op0=ALU.add, op1=ALU.add)), reads=[psb[slot], B["cst"]], writes=[CBBj[j]])
                        if Ws:
                            wb2 = cst_t[:, wo + 1:wo + 30].unsqueeze(1).broadcast_to([128, 16, 29])
                            S.op("dve", (lambda e: e.tensor_tensor(out=tmpc_a, in0=stc_t[:, j, 0:29, :].rearrange("p s b -> p b s"), in1=wb2, op=ALU.mult)),
                                 reads=[B["stc"], B["cst"]], writes=[tmpB])
                            S.op("dve", (lambda e: e.tensor_reduce(out=r1_a, in_=tmpc_a, axis=AX.X, op=ALU.add)), reads=[tmpB], writes=[tmpB])
                            S.op("dve", (lambda e: e.scalar_tensor_tensor(out=r1_a, in0=stc_t[:, j, 30, :], scalar=cst_t[:, wo:wo + 1], in1=r1_a,
                                                                          op0=ALU.mult, op1=ALU.add)), reads=[B["stc"], B["cst"], tmpB], writes=[tmpB])
                            S.op("dve", (lambda e: e.tensor_tensor(out=CB_a[:, j, WP:W], in0=CB_a[:, j, WP:W], in1=r1_a, op=ALU.add)), reads=[tmpB], writes=[CBBj[j]])

                    lnr = sig_a[0]

                    def ln_prep():
                        S.op("act", (lambda e: e.activation(out=sil_a[:, :, 0:W], in_=CB_a[:, :, 0:W], func=AF.Copy)), reads=CBBj, writes=[silB])

                    def ln_mean():
                        slot = mm_job(6, (lambda k: ones_t[:]), (lambda k: sil_a[:, k, 0:W]), [B["ones"], silB])
                        for j in range(6):
                            S.op("dve", (lambda e, j=j: e.scalar_tensor_tensor(out=v2(CB_a[:, j, 0:W]), in0=psv(slot), scalar=-1.0 / 768, in1=v2(CB_a[:, j, 0:W]),
                                                                                op0=ALU.mult, op1=ALU.add)), reads=[psb[slot]], writes=[CBBj[j]])
                        S.op("act", (lambda e: e.activation(out=sil_a[:, :, 0:W], in_=CB_a[:, :, 0:W], func=AF.Square)), reads=CBBj, writes=[silB])

                    def ln_var():
                        slot = mm_job(6, (lambda k: ones_t[:]), (lambda k: sil_a[:, k, 0:W]), [B["ones"], silB])
                        S.op("act", (lambda e: e.activation(out=v2(lnr[:, 0:W]), in_=psv(slot), func=AF.Ln, scale=1.0 / 768, bias=EPS)),
                             reads=[psb[slot]], writes=[sigB[0]])
                        S.op("act", (lambda e: e.activation(out=lnr[:, 0:W], in_=lnr[:, 0:W], func=AF.Exp, scale=-0.5)), reads=[sigB[0]], writes=[sigB[0]])
                        for j in range(6):
                            S.op("dve", (lambda e, j=j: e.tensor_tensor(out=CB_a[:, j, 0:W], in0=CB_a[:, j, 0:W], in1=lnr[:, 0:W], op=ALU.mult)), reads=[sigB[0]], writes=[CBBj[j]])
                            S.op("act", (lambda e, j=j: e.activation(out=sil_a[:, j, 0:W], in_=CB_a[:, j, 0:W], func=AF.Silu, scale=cs("lng", l * 6 + j), bias=cs("lnb", l * 6 + j))),
                                 reads=[CBBj[j], B["cst"]], writes=[silB])

                    def conf_pw_jobs():
                        for m in range(6):
                            wbuf, wv = ring.next(conf_pw, l * 6 + m, 0, 6)
                            slot = mm_job(6, (lambda k, wv=wv: wv[:, k, :]), (lambda k: sil_a[:, k, 0:W]), [wbuf, silB])
                            S.op("act", (lambda e, m=m, slot=slot: e.activation(out=v2(cat_t[:, 4 + m, 0:W]), in_=psv(slot), func=AF.Identity, bias=cs("pwb", l * 6 + m))),
                                 reads=[psb[slot], B["cst"]], writes=[B["cat"]])

                    def sc_triple(j):
                        i2 = j % 2
                        sc_ = zin(22 + j)
                        sh_ = zin(28 + j)
                        sbb = zin(16 + j)
                        hb, V, cv = hb_a[i2], V_a[i2], cv_a[i2]
                        wo = COFF["scw"] + (l * 6 + j) * 3
                        S.op("act", (lambda e: e.activation(out=v2(hb[:, 0:W]), in_=psv(sh_), func=AF.Copy)), reads=[psb[sh_]], writes=[hbB[i2]])
                        S.op("dve", (lambda e: e.tensor_copy(out=V[:, 0:2], in_=hist(OFF_S + 2 * j, 2))), reads=[hbuf], writes=[VB[i2]])
                        S.op("dve", (lambda e: e.tensor_tensor(out=v2(V[:, 2:2 + W]), in0=psv(sc_), in1=v2(hb[:, 0:W]), op=ALU.mult)),
                             reads=[psb[sc_], hbB[i2]], writes=[VB[i2]])
                        S.op("dve", (lambda e: e.tensor_copy(out=savedst(OFF_S + 2 * j, 2), in_=V[:, WP:WP + 2])), reads=[VB[i2]], writes=[sbuf_])
                        S.op("act", (lambda e: e.activation(out=cv[:, 0:W], in_=V[:, 2:2 + W], func=AF.Identity, scale=cst_t[:, wo + 2:wo + 3])),
                             reads=[VB[i2], B["cst"]], writes=[cvB[i2]])
                        for k in range(2):
                            S.op("dve", (lambda e, k=k: e.scalar_tensor_tensor(out=cv[:, 0:WP], in0=V[:, k:k + WP], scalar=cst_t[:, wo + k:wo + k + 1],
                                                                                in1=cv[:, 0:WP], op0=ALU.mult, op1=ALU.add)),
                                 reads=[VB[i2], B["cst"]], writes=[cvB[i2]])
                        if Ws:
                            S.op("dve", (lambda e: e.tensor_copy(out=sts_t[:, j, 1, :], in_=V[:, 2 + WP:2 + W])), reads=[VB[i2]], writes=[B["sts"]])
                            for (slot_k, k) in ((0, 1), (2, 0)):
                                S.op("dve", (lambda e, slot_k=slot_k, k=k: e.scalar_tensor_tensor(out=cv[:, WP:W], in0=sts_t[:, j, slot_k, :],
                                                                                                   scalar=cst_t[:, wo + k:wo + k + 1], in1=cv[:, WP:W],
                                                                                                   op0=ALU.mult, op1=ALU.add)),
                                     reads=[B["sts"], B["cst"]], writes=[cvB[i2]])
                        S.op("dve", (lambda e: e.tensor_tensor(out=v2(cat_t[:, 10 + j, 0:W]), in0=psv(sbb), in1=v2(cv[:, 0:W]), op=ALU.mult)),
                             reads=[psb[sbb], cvB[i2]], writes=[B["cat"]])

                    def pool_in(g):
                        i2 = g % 2
                        w = POOLW[g]
                        su = zin(g)
                        Ub, S1, S2, dbf = U_a[i2], S_a[0], S_a[1], dbf_a[i2]
                        S.op("dve", (lambda e: e.tensor_copy(out=Ub[:, 0:15], in_=hist(OFF_P + 15 * g, 15))), reads=[hbuf], writes=[UB[i2]])
                        S.op("act", (lambda e: e.activation(out=v2(Ub[:, 15:15 + W]), in_=psv(su), func=AF.Copy)), reads=[psb[su]], writes=[UB[i2]])
                        S.op("dve", (lambda e: e.tensor_copy(out=savedst(OFF_P + 15 * g, 15), in_=Ub[:, WP:WP + 15])), reads=[UB[i2]], writes=[sbuf_])
                        src, srcB = Ub, UB[i2]
                        E = 15 + WP
                        for i in range(g + 1):
                            dst, dstB = (S1, SB_[0]) if i % 2 == 0 else (S2, SB_[1])
                            sh = 1 << i
                            lo = 2 * sh - 1
                            S.op("dve", (lambda e, src=src, dst=dst, sh=sh, lo=lo: e.tensor_tensor(out=dst[:, lo:E], in0=src[:, lo:E], in1=src[:, lo - sh:E - sh], op=ALU.add)),
                                 reads=[srcB], writes=[dstB])
                            src, srcB = dst, dstB
                        S.op("dve", (lambda e: e.scalar_tensor_tensor(out=dbf[:, 0:WP], in0=src[:, 15:15 + WP], scalar=1.0 / w, in1=Ub[:, 15:15 + WP],
                                                                      op0=ALU.mult, op1=ALU.subtract)),
                             reads=[srcB, UB[i2]], writes=[dB[i2]])
                        if pi == 0:
                            S.op("dve", (lambda e: e.tensor_tensor(out=t16_a, in0=src[:, 15:31], in1=cs("invc", 16 * g, 16), op=ALU.mult)),
                                 reads=[srcB, B["cst"]], writes=[tmpB])
                            S.op("dve", (lambda e: e.tensor_tensor(out=dbf[:, 0:16], in0=t16_a, in1=Ub[:, 15:31], op=ALU.subtract)),
                                 reads=[tmpB, UB[i2]], writes=[dB[i2]])
                        if Ws:
                            S.op("dve", (lambda e: e.tensor_copy(out=stp_t[:, g, 14, :], in_=Ub[:, 15 + WP:15 + W])), reads=[UB[i2]], writes=[B["stp"]])
                            s0 = 0 if w == 16 else 15 - w
                            s1 = 16 if w == 16 else 15
                            S.op("dve", (lambda e: e.tensor_reduce(out=t16_a, in_=stp_t[:, g, s0:s1, :].rearrange("p s b -> p b s"), axis=AX.X, op=ALU.add)),
                                 reads=[B["stp"]], writes=[tmpB])
                            S.op("dve", (lambda e: e.scalar_tensor_tensor(out=dbf[:, WP:W], in0=t16_a, scalar=1.0 / w, in1=Ub[:, 15 + WP:15 + W],
                                                                          op0=ALU.mult, op1=ALU.subtract)),
                                 reads=[tmpB, UB[i2]], writes=[dB[i2]])

                    def pool_mm(g):
                        i2 = g % 2
                        dbf = dbf_a[i2]
                        slot = mm_job(1, (lambda k: wpl_t[:, g, :]), (lambda k: dbf[:, 0:W]), [B["wpl"], dB[i2]])
                        S.op("act", (lambda e: e.activation(out=v2(cat_t[:, g, 0:W]), in_=psv(slot), func=AF.Identity, scale=cs("pscale", l * 4 + g))),
                             reads=[psb[slot], B["cst"]], writes=[B["cat"]])

                    conf_dgen(0)
                    conf_pair(0)
                    for j in range(1, 6):
                        conf_pair(j)
                        conf_conv(j - 1)
                        conf_dgen(j)
                    sc_triple(0)
                    conf_conv(5)
                    ln_prep()
                    sc_triple(1)
                    ln_mean()
                    sc_triple(2)
                    ln_var()
                    sc_triple(3)
                    sc_triple(4)
                    sc_triple(5)
                    pool_in(0)
                    pool_in(1)
                    pool_mm(0)
                    pool_in(2)
                    pool_mm(1)
                    pool_in(3)
                    pool_mm(2)
                    conf_pw_jobs()
                    pool_mm(3)

                    warm_lnexp()
                    for m in range(KC):
                        slot = linear(w_out, l * KC + m, KC, cat_t, B["cat"])
                        S.op("dve", (lambda e, m=m, slot=slot: e.tensor_tensor(out=v2(h_t[:, m, 0:W]), in0=psv(slot), in1=v2(h_t[:, m, 0:W]), op=ALU.add)),
                             reads=[psb[slot]], writes=[hB[m]])
                        sq_chunk(m, "xn")

                    S.barrier(("act", "dve"))
                    btok = [(o, S.cnt[o]) for o in ("pe", "act", "dve")]
                    rmsnorm("nffn", l, "xn", presq=True)
                    ar = Arena()
                    ptw_a = ar.bf16(2 * D).rearrange("p (k m) -> p k m", k=2)
                    raw_a = [ar.f32(2 * (2 + WMAX)).rearrange("p (a w) -> p a w", a=2) for _ in range(2)]
                    acc_a = [ar.f32(2 * WMAX).rearrange("p (a w) -> p a w", a=2) for _ in range(2)]
                    sa_a = [ar.f32(WMAX) for _ in range(2)]
                    act_a = [ar.bf16(GS * WMAX).rearrange("p (j w) -> p j w", j=GS) for _ in range(2)]
                    rawB = [[Buf(f"raw{i}{ab}") for ab in range(2)] for i in range(2)]
                    accB = [[Buf(f"acc{i}{ab}") for ab in range(2)] for i in range(2)]
                    saB = [Buf("sa0"), Buf("sa1")]
                    actB = [Buf("act0"), Buf("act1")]
                    S.dma("pool", (lambda e: e.dma_start(out=ptw_a, in_=ple_proj[l * 256:(l + 1) * 256, :].rearrange("(k p) m -> p k m", p=128))),
                          "s_ptw", writes=[B["ptw"]], extra=btok)

                    def down_proj(q):
                        qa = q % 2
                        for m in range(KC):
                            wbuf, wv = ring.next(w_down, l * KC + m, q * GS, GS)
                            slot = mm_job(GS, (lambda k, wv=wv: wv[:, k, :]), (lambda k, qa=qa: act_a[qa][:, k, 0:W]), [wbuf, actB[qa]])
                            S.op("dve", (lambda e, m=m, slot=slot: e.tensor_tensor(out=v2(h_t[:, m, 0:W]), in0=psv(slot), in1=v2(h_t[:, m, 0:W]), op=ALU.add)),
                                 reads=[psb[slot]], writes=[hB[m]])
                            if q == FC // GS - 1:
                                xnp_chunk(m, "nple", l)
                                sq_chunk(m, "cat")

                    def ffn_tail(j):
                        i2, qa, jj = j % 2, (j // GS) % 2, j % GS
                        acc, sa = acc_a[i2], sa_a[i2]
                        S.op("act", (lambda e: e.activation(out=sa[:, 0:W], in_=acc[:, 0, 0:W], func=AF.Silu)), reads=[accB[i2][0]], writes=[saB[i2]])
                        S.op("dve", (lambda e: e.tensor_tensor(out=act_a[qa][:, jj, 0:W], in0=sa[:, 0:W], in1=acc[:, 1, 0:W], op=ALU.mult)),
                             reads=[saB[i2], accB[i2][1]], writes=[actB[qa]])

                    preu = dict(zip(((0, 0), (0, 1), (1, 0)),
                                    linear_multi([(w_up, l * 2 * FC + 0), (w_up, l * 2 * FC + FC), (w_up, l * 2 * FC + 1)])))
                    for q in range(FC // GS):
                        for jj in range(GS):
                            if q > 0 and jj == 2 and q < FC // GS - 1:
                                down_proj(q - 1)
                            j = q * GS + jj
                            i2 = j % 2
                            raw, acc = raw_a[i2], acc_a[i2]
                            wo = COFF["ffw"] + (l * FC + j) * 6
                            S.op("dve", (lambda e: e.tensor_copy(out=raw[:, :, 0:2], in_=hist(OFF_F + 4 * j, 4).rearrange("p (a k) -> p a k", a=2))),
                                 reads=[hbuf], writes=rawB[i2])
                            for ab in range(2):
                                s_ = preu.pop((j, ab)) if (j, ab) in preu else linear(w_up, l * 2 * FC + ab * FC + j, KC, xn_t, "xn")
                                S.op("act", (lambda e: e.activation(out=v2(raw[:, ab, 2:2 + W]), in_=psv(s_), func=AF.Copy)),
                                     reads=[psb[s_]], writes=[rawB[i2][ab]])
                                S.op("act", (lambda e: e.activation(out=v2(acc[:, ab, 0:W]), in_=psv(s_), func=AF.Identity,
                                                                    scale=cst_t[:, wo + 3 * ab + 2:wo + 3 * ab + 3])),
                                     reads=[psb[s_], B["cst"]], writes=[accB[i2][ab]])
                                for k in range(2):
                                    S.op("dve", (lambda e, k=k: e.scalar_tensor_tensor(out=acc[:, ab, 0:WP], in0=raw[:, ab, k:k + WP],
                                                                                        scalar=cst_t[:, wo + 3 * ab + k:wo + 3 * ab + k + 1],
                                                                                        in1=acc[:, ab, 0:WP], op0=ALU.mult, op1=ALU.add)),
                                         reads=[rawB[i2][ab], B["cst"]], writes=[accB[i2][ab]])
                                if Ws:
                                    for (slot_k, k) in ((0, 1), (2, 0)):
                                        S.op("dve", (lambda e, slot_k=slot_k, k=k: e.scalar_tensor_tensor(
                                            out=acc[:, ab, WP:W], in0=stf_t[:, j, ab, slot_k, :], scalar=cst_t[:, wo + 3 * ab + k:wo + 3 * ab + k + 1],
                                            in1=acc[:, ab, WP:W], op0=ALU.mult, op1=ALU.add)),
                                            reads=[B["stf"], B["cst"]], writes=[accB[i2][ab]])
                            S.op("dve", (lambda e: e.tensor_copy(out=savedst(OFF_F + 4 * j, 4).rearrange("p (a k) -> p a k", a=2), in_=raw[:, :, WP:WP + 2])),
                                 reads=rawB[i2], writes=[sbuf_])
                            if Ws:
                                S.op("dve", (lambda e: e.tensor_copy(out=stf_t[:, j, :, 1, :], in_=raw[:, :, 2 + WP:2 + W])), reads=rawB[i2], writes=[B["stf"]])
                            if j > 0:
                                ffn_tail(j - 1)
                    ffn_tail(FC - 1)
                    down_proj(FC // GS - 2)
                    warm_lnexp()
                    down_proj(FC // GS - 1)

                    S.barrier(("act", "dve"))
                    st["fresh_xn"] = True
                    ar = Arena()
                    ptw_a = ar.bf16(2 * D).rearrange("p (k m) -> p k m", k=2)
                    gs_a = [ar.f32(WMAX) for _ in range(2)]
                    pg_a = [ar.f32(WMAX) for _ in range(2)]
                    gsB = [Buf("gs0"), Buf("gs1")]
                    pgB = [Buf("pg0"), Buf("pg1")]
                    for m in range(KC):
                        i2 = m % 2
                        sg_ = linear(ple_gate, l * KC + m, KC, xn_t, "xn")
                        if m == 0:
                            rmsnorm_stats("cat")
                        S.op("dve", (lambda e, sg_=sg_, i2=i2: e.tensor_tensor(out=v2(gs_a[i2][:, 0:W]), in0=psv(sg_), in1=v2(rstd_t[:, 0:W]), op=ALU.mult)),
                             reads=[psb[sg_], B["rstd"]], writes=[gsB[i2]])
                        S.op("act", (lambda e, i2=i2: e.activation(out=gs_a[i2][:, 0:W], in_=gs_a[i2][:, 0:W], func=AF.Sigmoid)), reads=[gsB[i2]], writes=[gsB[i2]])
                        if m == KC - 1:
                            warm_lnexp()
                        sp_ = mm_job(2, (lambda k, m=m: ptw_a[:, k, m * 128:(m + 1) * 128]), (lambda k: ptb_t[:, k, 0:W]), [B["ptw"], B["ptb"]])
                        S.op("dve", (lambda e, sp_=sp_, i2=i2: e.tensor_tensor(out=v2(pg_a[i2][:, 0:W]), in0=psv(sp_), in1=v2(gs_a[i2][:, 0:W]), op=ALU.mult)),
                             reads=[psb[sp_], gsB[i2]], writes=[pgB[i2]])
                        S.op("dve", (lambda e, m=m, i2=i2: e.tensor_tensor(out=h_t[:, m, 0:W], in0=h_t[:, m, 0:W], in1=pg_a[i2][:, 0:W], op=ALU.add)),
                             reads=[pgB[i2]], writes=[hB[m]])
                        sq_chunk(m, "cat")
                    S.barrier(("act", "dve"))

                    if pi == 1:
                        S.dma("sp", (lambda e, l=l: e.dma_start(out=pst_out[l], in_=ost_t[:, l, :])), f"s_ost{l}", reads=[B[f"ost{l}"]])
                        S.dma("sp", (lambda e, l=l: e.dma_start(out=op_out[l], in_=stp_t[:].rearrange("p a b c -> p (a b c)"))), "s_stp", reads=[B["stp"]])
                        S.dma("sp", (lambda e, l=l: e.dma_start(out=oc_out[l], in_=stc_t[:].rearrange("p a b c -> p (a b c)"))), "s_stc", reads=[B["stc"]])
                        S.dma("sp", (lambda e, l=l: e.dma_start(out=os_out[l], in_=sts_t[:].rearrange("p a b c -> p (a b c)"))), "s_sts", reads=[B["sts"]])
                        S.dma("sp", (lambda e, l=l: e.dma_start(out=of_out[l], in_=stf_t[:].rearrange("p a b c d -> p (a b c d)"))), "s_stf", reads=[B["stf"]])

                rmsnorm("nfin", 0, "cat", presq=True, final=True)
                S.dma("sp", (lambda e: e.dma_start(out=yT[:, 0:8, c0:c0 + WP], in_=yv0[:, :, 0:WP])), "s_y0", reads=xnB)
                if Ws:
                    S.dma("sp", (lambda e: e.dma_start(out=yT[:, 0:8, TPC:TC], in_=yv0[:, :, WP:W])), "s_y0", reads=xnB)
                S.dma("sp", (lambda e: e.dma_start(out=yT[:, 8:16, c0:c0 + WP], in_=yv1[:, :, 0:WP])), "s_y1", reads=[B["cat"]])
                if Ws:
                    S.dma("sp", (lambda e: e.dma_start(out=yT[:, 8:16, TPC:TC], in_=yv1[:, :, WP:W])), "s_y1", reads=[B["cat"]])

            fin = [(s, S.cnt[s]) for s in ("s_y0", "s_y1", "s_ost0", "s_ost1", "s_stp", "s_stc", "s_sts", "s_stf")]
            S.wait("sp", fin)
            assert ring.cur == len(ring.jobs)

        S0 = Sched()
        program(S0)
        ring.dry = False
        S = Sched()
        program(S)

        @block.tensor
        def _(e):
            S.emit("pe", e, sems)

        @block.scalar
        def _(e):
            S.emit("act", e, sems)

        @block.vector
        def _(e):
            S.emit("dve", e, sems)

        @block.gpsimd
        def _(e):
            S.emit("pool", e, sems)

        @block.sync
        def _(e):
            S.emit("sp", e, sems)
    return nc


def _fm(X):
    ncol, nf = X.shape
    return np.ascontiguousarray(X.T.reshape(nf // 128, 128, ncol).transpose(1, 0, 2))


def _unfm(A):
    p, k, n = A.shape
    return np.ascontiguousarray(A.transpose(2, 1, 0).reshape(n, k * 128))


def _vec(v):
    return np.ascontiguousarray(v.reshape(-1, 128).T)


def prepare_inputs(inp):
    f = lambda a: np.asarray(a, dtype=np.float32)
    x_prompt, x_sample = f(inp["x_prompt"]), f(inp["x_sample"])
    p_prompt, p_sample = f(inp["p_prompt"]), f(inp["p_sample"])
    st_pool, st_conf, st_sc, st_ffn = f(inp["state_pool"]), f(inp["state_conf"]), f(inp["state_sc"]), f(inp["state_ffn"])
    def tiles(w):
        l_, k_, m_ = w.shape
        return np.ascontiguousarray(w.reshape(l_, k_ // 128, 128, m_ // 128, 128).transpose(0, 3, 2, 1, 4)).reshape(l_ * (m_ // 128), 128, k_)
    shared = {
        "w_in": tiles(f(inp["w_in"])),
        "w_pool": f(inp["w_pool"]).reshape(L * 512, 128),
        "conf_pw": tiles(f(inp["conf_pw"])),
        "w_out": tiles(f(inp["w_out"])),
        "w_up": tiles(f(inp["w_up"])),
        "w_down": tiles(f(inp["w_down"])),
        "ple_gate": tiles(f(inp["ple_gate"])),
        "ple_proj": f(inp["ple_proj"]).reshape(L * 256, D),
    }
    cbase = np.zeros((128, NCST), np.float32)

    def put(name, arr):
        arr = arr.reshape(128, -1)
        cbase[:, COFF[name]:COFF[name] + arr.shape[1]] = arr
    put("ident", np.eye(128, dtype=np.float32))
    put("nmix", np.stack([_vec(f(inp["norm_mix"])[l]) for l in range(L)], 1))
    put("nffn", np.stack([_vec(f(inp["norm_ffn"])[l]) for l in range(L)], 1))
    put("nple", np.stack([_vec(f(inp["norm_ple"])[l]) for l in range(L)], 1))
    put("nfin", _vec(f(inp["norm_final"])))
    put("pscale", np.stack([_vec(f(inp["pool_scale"])[l]) for l in range(L)], 1))
    put("cdw", np.stack([f(inp["conf_dw"])[l].reshape(31, 6, 128).transpose(2, 1, 0) for l in range(L)], 1))
    put("cdb", np.stack([_vec(f(inp["conf_dw_b"])[l]) for l in range(L)], 1))
    put("lng", np.stack([_vec(f(inp["conf_ln_g"])[l]) for l in range(L)], 1))
    put("lnb", np.stack([_vec(f(inp["conf_ln_b"])[l]) for l in range(L)], 1))
    put("pwb", np.stack([_vec(f(inp["conf_pw_b"])[l]) for l in range(L)], 1))
    put("scw", np.stack([f(inp["sc_conv"])[l].reshape(3, 6, 128).transpose(2, 1, 0) for l in range(L)], 1))
    put("ffw", np.stack([f(inp["ffn_conv"])[l].reshape(3, 2, FC, 128).transpose(3, 2, 1, 0) for l in range(L)], 1))

    def hist_layout(stl, nch, kh):
        a = stl.transpose(2, 1, 0).reshape(nch, 128, kh, 16).transpose(1, 0, 2, 3)
        out = np.zeros((128, nch, kh + 1, 16), np.float32)
        out[:, :, 0:kh - 1, :] = a[:, :, 1:kh, :]
        out[:, :, kh, :] = a[:, :, 0, :]
        return out

    in_maps = []
    for c in range(NCORE):
        b, half = c // 2, c % 2
        t0 = 0 if half == 0 else 2048 - TPC
        sl = slice(NS * c, NS * (c + 1))
        m = dict(shared)
        m["xT"] = _fm(np.concatenate([x_prompt[b, t0:t0 + TPC], x_sample[sl, 0]], 0))
        m["pT"] = np.stack([_fm(np.concatenate([p_prompt[l, b, t0:t0 + TPC], p_sample[l, sl, 0]], 0)) for l in range(L)], 0)
        m["hp"] = np.stack([hist_layout(st_pool[l, sl], 4, 15) for l in range(L)], 0).reshape(L, 128, -1)
        m["hc"] = np.stack([hist_layout(st_conf[l, sl], 6, 30) for l in range(L)], 0).reshape(L, 128, -1)
        m["hs"] = np.stack([hist_layout(st_sc[l, sl], 6, 2) for l in range(L)], 0).reshape(L, 128, -1)
        hf = []
        for l in range(L):
            a = hist_layout(st_ffn[l, sl], 2 * FC, 2)
            hf.append(a.reshape(128, 2, FC, 3, 16).transpose(0, 2, 1, 3, 4))
        m["hf"] = np.ascontiguousarray(np.stack(hf, 0)).reshape(L, 128, -1)
        cc = cbase.copy()
        invc = np.zeros((4, 16), np.float32)
        for g, w in enumerate(POOLW):
            for i in range(16):
                pos = t0 + i
                invc[g, i] = 1.0 / min(w, pos + 1)
        cc[:, COFF["invc"]:COFF["invc"] + 64] = invc.reshape(1, 64)
        m["cst"] = cc
        in_maps.append(m)
    return in_maps


def assemble(results):
    y_prompt = np.zeros((4, 2048, D), np.float32)
    y_sample = np.zeros((128, 1, D), np.float32)
    npp = np.zeros((L, 4, 15, 512), np.float32)
    nps = np.zeros((L, 128, 15, 512), np.float32)
    ncp = np.zeros((L, 4, 30, 768), np.float32)
    ncs = np.zeros((L, 128, 30, 768), np.float32)
    nsp = np.zeros((L, 4, 2, 768), np.float32)
    nss = np.zeros((L, 128, 2, 768), np.float32)
    nfp = np.zeros((L, 4, 2, 2 * DFF), np.float32)
    nfs = np.zeros((L, 128, 2, 2 * DFF), np.float32)
    for c in range(NCORE):
        r = results[c]
        b, half = c // 2, c % 2
        sl = slice(NS * c, NS * (c + 1))
        Y = _unfm(np.asarray(r["yT"]))
        if half == 0:
            y_prompt[b, 0:TPC] = Y[0:TPC]
        else:
            y_prompt[b, TPC:2048] = Y[2 * TPC - 2048:TPC]
        y_sample[sl, 0] = Y[TPC:TC]
        pst = np.asarray(r["pst"])
        ops_ = np.asarray(r["ops"]).reshape(L, 128, 4, 16, 16)
        ocs_ = np.asarray(r["ocs"]).reshape(L, 128, 6, 31, 16)
        oss_ = np.asarray(r["oss"]).reshape(L, 128, 6, 3, 16)
        ofs_ = np.asarray(r["ofs"]).reshape(L, 128, FC, 2, 3, 16)
        for l in range(L):
            if half == 1:
                npp[l, b] = pst[l][:, OFF_P:OFF_P + 60].reshape(128, 4, 15).transpose(2, 1, 0).reshape(15, 512)
                ncp[l, b] = pst[l][:, OFF_C:OFF_C + 180].reshape(128, 6, 30).transpose(2, 1, 0).reshape(30, 768)
                nsp[l, b] = pst[l][:, OFF_S:OFF_S + 12].reshape(128, 6, 2).transpose(2, 1, 0).reshape(2, 768)
                nfp[l, b] = pst[l][:, OFF_F:OFF_F + 176].reshape(128, FC, 2, 2).transpose(3, 2, 1, 0).reshape(2, 2 * DFF)
            nps[l, sl] = ops_[l][:, :, 0:15, :].transpose(3, 2, 1, 0).reshape(16, 15, 512)
            ncs[l, sl] = ocs_[l][:, :, 0:30, :].transpose(3, 2, 1, 0).reshape(16, 30, 768)
            nss[l, sl] = oss_[l][:, :, 0:2, :].transpose(3, 2, 1, 0).reshape(16, 2, 768)
            nfs[l, sl] = ofs_[l][:, :, :, 0:2, :].transpose(4, 3, 2, 1, 0).reshape(16, 2, 2 * DFF)
    return (y_prompt, y_sample, npp, nps, ncp, ncs, nsp, nss, nfp, nfs)


_NC = None


def kernel(**inputs):
    global _NC
    if _NC is None:
        _NC = build_nc()
    in_maps = prepare_inputs(inputs)
    res = run_bass_kernel_spmd(_NC, in_maps, core_ids=list(range(NCORE)))
    return assemble(res.results)
```
